# Optimizing a Trainium2 kernel written in Bass

```python
import math
import jax, jax.numpy as jnp
from jax import lax
import numpy as np

D_MODEL = 1024
BATCH = 2
SEQ = 8192
DEPTH = 4

GRID_W = 64
CTX_LEN = 256
N_MIXERS = 2
N_MLA_LAYERS = (DEPTH + N_MIXERS - 1) // N_MIXERS
N_POOL_LAYERS = DEPTH // N_MIXERS
MLA_HEADS = 8
QK_NOPE_DIM = 128
QK_ROPE_DIM = 64
V_HEAD_DIM = 128
Q_LORA_RANK = 384
KV_LORA_RANK = 256
MLA_IN_DIM = Q_LORA_RANK + KV_LORA_RANK + QK_ROPE_DIM
ROPE_THETA = 10000.0
Q_BLOCK = 128
POOL_WINDOWS = (2, 4, 8, 16)
N_POOL_GROUPS = 4
POOL_GROUP_DIM = D_MODEL // N_POOL_GROUPS
PEER_HEADS = 8
PEER_N_KEYS = 128
PEER_N_EXPERTS = PEER_N_KEYS * PEER_N_KEYS
PEER_QUERY_DIM = 256
PEER_HALF = PEER_QUERY_DIM // 2
PEER_TOPK = 16
PEER_BLOCK = 128
LN_EPS = 1e-5
RMS_EPS = 1e-6
DEEPNORM_ALPHA = (2.0 * DEPTH) ** 0.25
DEEPNORM_BETA = (8.0 * DEPTH) ** -0.25
N_MOD = 6

kernel_name = "hybrid_mla_pool_peer_prefix_dit"


def _layernorm(x, g, b):
    xf = x.astype(jnp.float32)
    mu = jnp.mean(xf, axis=-1, keepdims=True)
    var = jnp.mean(jnp.square(xf - mu), axis=-1, keepdims=True)
    return ((xf - mu) * lax.rsqrt(var + LN_EPS) * g + b).astype(x.dtype)


def _rmsnorm(x, g):
    xf = x.astype(jnp.float32)
    return (xf * lax.rsqrt(jnp.mean(jnp.square(xf), axis=-1, keepdims=True) + RMS_EPS) * g).astype(x.dtype)


def _axial_rope_tables(n_tokens):
    rows_n = n_tokens // GRID_W
    row = jnp.repeat(jnp.arange(rows_n), GRID_W).astype(jnp.float32)
    col = jnp.tile(jnp.arange(GRID_W), rows_n).astype(jnp.float32)
    n_freq = QK_ROPE_DIM // 4
    freqs = ROPE_THETA ** (-jnp.arange(n_freq, dtype=jnp.float32) / n_freq)
    ang = jnp.concatenate([row[:, None] * freqs, col[:, None] * freqs], axis=-1)
    return jnp.cos(ang), jnp.sin(ang)


def _apply_rope(x, cos, sin):
    half = x.shape[-1] // 2
    xf = x.astype(jnp.float32)
    x1, x2 = xf[..., :half], xf[..., half:]
    return jnp.concatenate([x1 * cos - x2 * sin, x1 * sin + x2 * cos], axis=-1).astype(x.dtype)


def _block_attention(q, k, v):
    b, s, h, dk = q.shape
    scale = dk ** -0.5
    qb = q.reshape(b, s // Q_BLOCK, Q_BLOCK, h, dk).transpose(1, 0, 2, 3, 4)

    def one(qblk):
        sc = jnp.einsum('bqhd,bkhd->bhqk', qblk, k, preferred_element_type=jnp.float32) * scale
        p = jax.nn.softmax(sc, axis=-1).astype(v.dtype)
        return jnp.einsum('bhqk,bkhd->bqhd', p, v)

    o = lax.map(one, qb)
    return o.transpose(1, 0, 2, 3, 4).reshape(b, s, h * v.shape[-1])


def _mla_qkv(h, rope, w_in, q_norm_g, kv_norm_g, w_uq, w_ukv, with_queries):
    b, l, _ = h.shape
    z = h @ w_in
    c_q = z[..., :Q_LORA_RANK]
    c_kv = z[..., Q_LORA_RANK:Q_LORA_RANK + KV_LORA_RANK]
    k_rope = z[..., Q_LORA_RANK + KV_LORA_RANK:]
    kv = (_rmsnorm(c_kv, kv_norm_g) @ w_ukv).reshape(b, l, MLA_HEADS, QK_NOPE_DIM + V_HEAD_DIM)
    k_nope, v = kv[..., :QK_NOPE_DIM], kv[..., QK_NOPE_DIM:]
    if rope is not None:
        cos, sin = rope
        k_rope = _apply_rope(k_rope, cos, sin)
    k = jnp.concatenate([k_nope, jnp.broadcast_to(k_rope[:, :, None, :], (b, l, MLA_HEADS, QK_ROPE_DIM))], axis=-1)
    if not with_queries:
        return None, k, v
    q = (_rmsnorm(c_q, q_norm_g) @ w_uq).reshape(b, l, MLA_HEADS, QK_NOPE_DIM + QK_ROPE_DIM)
    q_nope, q_rope = q[..., :QK_NOPE_DIM], q[..., QK_NOPE_DIM:]
    if rope is not None:
        q_rope = _apply_rope(q_rope, cos[:, None, :], sin[:, None, :])
    return jnp.concatenate([q_nope, q_rope], axis=-1), k, v


def _centred_mean_minus_self(x, window):
    b, l, ch = x.shape
    xf = x.astype(jnp.float32)
    cs = jnp.concatenate([jnp.zeros((b, 1, ch), jnp.float32), jnp.cumsum(xf, axis=1)], axis=1)
    t = jnp.arange(l)
    lo = jnp.clip(t - window // 2, 0, l)
    hi = jnp.clip(t + window // 2, 0, l)
    tot = jnp.take(cs, hi, axis=1) - jnp.take(cs, lo, axis=1)
    cnt = (hi - lo).astype(jnp.float32)[None, :, None]
    return (tot / cnt - xf).astype(x.dtype)


def _pool_mixer(h, w_in, w_grp, scale, w_out):
    b, l, _ = h.shape
    zg = (h @ w_in).reshape(b, l, N_POOL_GROUPS, POOL_GROUP_DIM)
    pooled = jnp.stack([_centred_mean_minus_self(zg[:, :, g], POOL_WINDOWS[g]) for g in range(N_POOL_GROUPS)], axis=2)
    y = jnp.einsum('blgc,gcd->blgd', pooled, w_grp).reshape(b, l, D_MODEL) * scale
    return y @ w_out


def _peer(h, w_q, k1, k2, u_tab, v_tab):
    b, l, d = h.shape
    q = (h @ w_q).reshape(b, l, PEER_HEADS, 2, PEER_HALF)
    s1 = jnp.einsum('blhc,hnc->blhn', q[..., 0, :], k1, preferred_element_type=jnp.float32)
    s2 = jnp.einsum('blhc,hnc->blhn', q[..., 1, :], k2, preferred_element_type=jnp.float32)
    v1, i1 = lax.top_k(s1, PEER_TOPK)
    v2, i2 = lax.top_k(s2, PEER_TOPK)
    cand_s = (v1[..., :, None] + v2[..., None, :]).reshape(b, l, PEER_HEADS, PEER_TOPK * PEER_TOPK)
    cand_i = (i1[..., :, None] * PEER_N_KEYS + i2[..., None, :]).reshape(b, l, PEER_HEADS, PEER_TOPK * PEER_TOPK)
    top_s, pos = lax.top_k(cand_s, PEER_TOPK)
    idx = jnp.take_along_axis(cand_i, pos, axis=-1)
    g = jax.nn.softmax(top_s, axis=-1).astype(h.dtype)
    n_blk = (b * l) // PEER_BLOCK
    n_sel = PEER_HEADS * PEER_TOPK
    hb = h.reshape(n_blk, PEER_BLOCK, d)
    ib = idx.reshape(n_blk, PEER_BLOCK, n_sel)
    gb = g.reshape(n_blk, PEER_BLOCK, n_sel)

    def one(args):
        hx, ix, gx = args
        u = jnp.take(u_tab, ix, axis=0)
        a = jax.nn.gelu(jnp.einsum('ted,td->te', u, hx), approximate=False)
        vv = jnp.take(v_tab, ix, axis=0)
        return jnp.einsum('te,ted->td', gx * a, vv)

    return lax.map(one, (hb, ib, gb)).reshape(b, l, d)


def _post_norm(x, y, gate, g, b):
    return _layernorm(DEEPNORM_ALPHA * x + gate * y, g, b)


def setup_inputs(seed: int = 0) -> dict:
    key = jax.random.key(seed)
    ks = jax.random.split(key, 24)
    nrm = lambda k, shp, s: jax.random.normal(k, shp, jnp.float32) * s
    D = D_MODEL
    return {
        "x": nrm(ks[0], (BATCH, SEQ, D), 1.0),
        "c": nrm(ks[1], (BATCH, D), 1.0),
        "ctx": nrm(ks[2], (BATCH, CTX_LEN, D), 1.0),
        "c_ctx": nrm(ks[3], (D,), 1.0),
        "w_mod": nrm(ks[4], (DEPTH, D, N_MOD * D), 0.5 * D ** -0.5),
        "b_mod": nrm(ks[5], (DEPTH, N_MOD * D), 0.02),
        "ln_g": 1.0 + nrm(ks[6], (DEPTH, 2, D), 0.05),
        "ln_b": nrm(ks[7], (DEPTH, 2, D), 0.02),
        "mla_w_in": nrm(ks[8], (N_MLA_LAYERS, D, MLA_IN_DIM), D ** -0.5),
        "mla_q_norm": 1.0 + nrm(ks[9], (N_MLA_LAYERS, Q_LORA_RANK), 0.05),
        "mla_kv_norm": 1.0 + nrm(ks[10], (N_MLA_LAYERS, KV_LORA_RANK), 0.05),
        "mla_w_uq": nrm(ks[11], (N_MLA_LAYERS, Q_LORA_RANK, MLA_HEADS * (QK_NOPE_DIM + QK_ROPE_DIM)), Q_LORA_RANK ** -0.5),
        "mla_w_ukv": nrm(ks[12], (N_MLA_LAYERS, KV_LORA_RANK, MLA_HEADS * (QK_NOPE_DIM + V_HEAD_DIM)), KV_LORA_RANK ** -0.5),
        "mla_w_o": nrm(ks[13], (N_MLA_LAYERS, MLA_HEADS * V_HEAD_DIM, D), DEEPNORM_BETA * (MLA_HEADS * V_HEAD_DIM) ** -0.5),
        "pool_w_in": nrm(ks[14], (N_POOL_LAYERS, D, D), D ** -0.5),
        "pool_w_grp": nrm(ks[15], (N_POOL_LAYERS, N_POOL_GROUPS, POOL_GROUP_DIM, POOL_GROUP_DIM), POOL_GROUP_DIM ** -0.5),
        "pool_scale": 1.0 + nrm(ks[16], (N_POOL_LAYERS, D), 0.1),
        "pool_w_out": nrm(ks[17], (N_POOL_LAYERS, D, D), DEEPNORM_BETA * D ** -0.5),
        "peer_w_q": nrm(ks[18], (DEPTH, D, PEER_HEADS * PEER_QUERY_DIM), D ** -0.5),
        "peer_k1": nrm(ks[19], (DEPTH, PEER_HEADS, PEER_N_KEYS, PEER_HALF), PEER_HALF ** -0.5),
        "peer_k2": nrm(ks[20], (DEPTH, PEER_HEADS, PEER_N_KEYS, PEER_HALF), PEER_HALF ** -0.5),
        "peer_u": nrm(ks[21], (DEPTH, PEER_N_EXPERTS, D), D ** -0.5),
        "peer_v": nrm(ks[22], (DEPTH, PEER_N_EXPERTS, D), DEEPNORM_BETA),
    }


def reference(x, c, ctx, c_ctx, w_mod, b_mod, ln_g, ln_b, mla_w_in, mla_q_norm, mla_kv_norm, mla_w_uq, mla_w_ukv, mla_w_o,
              pool_w_in, pool_w_grp, pool_scale, pool_w_out, peer_w_q, peer_k1, peer_k2, peer_u, peer_v):
    b, s, _ = x.shape
    rope = _axial_rope_tables(s)
    silu_c = jax.nn.silu(c)
    silu_ctx = jax.nn.silu(c_ctx)
    for i in range(DEPTH):
        mixer = i % N_MIXERS
        j = i // N_MIXERS
        ctx_needed = any(k % N_MIXERS == 0 for k in range(i + 1, DEPTH))
        ml = (silu_c @ w_mod[i] + b_mod[i])[:, None, :]
        mc = silu_ctx @ w_mod[i] + b_mod[i]
        sh1, sc1, g1, sh2, sc2, g2 = jnp.split(ml, N_MOD, axis=-1)
        csh1, csc1, cg1, csh2, csc2, cg2 = jnp.split(mc, N_MOD, axis=-1)
        hx = x * (1.0 + sc1) + sh1
        hc = ctx * (1.0 + csc1) + csh1
        if mixer == 0:
            w = (mla_w_in[j], mla_q_norm[j], mla_kv_norm[j], mla_w_uq[j], mla_w_ukv[j])
            q, k, v = _mla_qkv(hx, rope, *w, True)
            qc, kc, vc = _mla_qkv(hc, None, *w, ctx_needed)
            y = _block_attention(q, jnp.concatenate([k, kc], axis=1), jnp.concatenate([v, vc], axis=1)) @ mla_w_o[j]
            if ctx_needed:
                yc = _block_attention(qc, kc, vc) @ mla_w_o[j]
        else:
            w = (pool_w_in[j], pool_w_grp[j], pool_scale[j], pool_w_out[j])
            y = _pool_mixer(hx, *w)
            if ctx_needed:
                yc = _pool_mixer(hc, *w)
        pw = (peer_w_q[i], peer_k1[i], peer_k2[i], peer_u[i], peer_v[i])
        x = _post_norm(x, y, g1, ln_g[i, 0], ln_b[i, 0])
        x = _post_norm(x, _peer(x * (1.0 + sc2) + sh2, *pw), g2, ln_g[i, 1], ln_b[i, 1])
        if ctx_needed:
            ctx = _post_norm(ctx, yc, cg1, ln_g[i, 0], ln_b[i, 0])
            ctx = _post_norm(ctx, _peer(ctx * (1.0 + csc2) + csh2, *pw), cg2, ln_g[i, 1], ln_b[i, 1])
    return x
```

```python
import math
from contextlib import ExitStack

import numpy as np
import concourse.bass as bass
import concourse.mybir as mybir
from concourse.bass_utils import run_bass_kernel_spmd

F32 = mybir.dt.float32
BF16 = mybir.dt.bfloat16
U32 = mybir.dt.uint32
AF = mybir.ActivationFunctionType
ALU = mybir.AluOpType
AX = mybir.AxisListType

D = 1024
DEPTH = 4
SEQ = 8192
CTX = 256
NCORE = 8
TOK = 2048
NT = TOK // 128
NKT = (SEQ + CTX) // 128
ALPHA = (2.0 * DEPTH) ** 0.25
LN_EPS = 1e-5
RMS_EPS = 1e-6
NU = 8


class Buf:
    __slots__ = ("last_w", "readers")

    def __init__(self):
        self.last_w = None
        self.readers = {}


class FW:
    ENG = ("pe", "dve", "act", "pool", "sp")
    PAGE = 24000

    def __init__(self, nc, es, n_dma_sems=64):
        self.nc = nc
        self.es = es
        self.eng = {"pe": nc.tensor, "dve": nc.vector, "act": nc.scalar, "pool": nc.gpsimd, "sp": nc.sync}
        self.nsem = 0
        self.sem = {k: self._newsem() for k in self.ENG}
        self.cnt = {k: 0 for k in self.ENG}
        self.last = {k: None for k in self.ENG}
        self.known = {k: {} for k in self.ENG}
        self.dsems = []
        self.dcnt = []
        self.dall = []
        self.bufs = {}
        self.ninst = 0

    def _newsem(self):
        self.nsem += 1
        return self.es.enter_context(self.nc.semaphore("fs%d" % self.nsem))

    def buf(self, name):
        b = self.bufs.get(name)
        if b is None:
            b = Buf()
            self.bufs[name] = b
        return b

    def dsem(self):
        self.dsems.append(self._newsem())
        self.dcnt.append(0)
        return len(self.dsems) - 1

    def _deps(self, ek, reads, writes, same_ok):
        deps = {}
        for b in reads:
            t = self.buf(b).last_w
            if t is not None and deps.get(t[0], 0) < t[1]:
                deps[t[0]] = t[1]
        for b in writes:
            bb = self.buf(b)
            t = bb.last_w
            if t is not None and deps.get(t[0], 0) < t[1]:
                deps[t[0]] = t[1]
            for s, v in bb.readers.items():
                if deps.get(s, 0) < v:
                    deps[s] = v
        eng = self.eng[ek]
        kn = self.known[ek]
        own = self.sem[ek]
        for s, v in deps.items():
            if same_ok and s is own:
                continue
            if kn.get(s, 0) < v:
                eng.wait_ge(s, v)
                kn[s] = v
                self.ninst += 1

    def _record(self, tok, reads, writes):
        s, v = tok
        for b in reads:
            bb = self.buf(b)
            if bb.readers.get(s, 0) < v:
                bb.readers[s] = v
        for b in writes:
            bb = self.buf(b)
            bb.last_w = tok
            bb.readers = {}

    def op(self, ek, fn, reads=(), writes=(), same_ok=False):
        if self.cnt[ek] >= self.PAGE:
            self.sem[ek] = self._newsem()
            self.cnt[ek] = 0
        self._deps(ek, reads, writes, same_ok)
        inst = fn()
        self.cnt[ek] += 1
        inst.then_inc(self.sem[ek], 1)
        self.ninst += 1
        tok = (self.sem[ek], self.cnt[ek])
        self.last[ek] = tok
        self._record(tok, reads, writes)

    def dma(self, ek, fn, di, reads=(), writes=(), sync=False):
        if self.dcnt[di] >= self.PAGE:
            self.dall.append((self.dsems[di], self.dcnt[di]))
            self.dsems[di] = self._newsem()
            self.dcnt[di] = 0
        self._deps(ek, reads, writes, False)
        inst = fn()
        self.dcnt[di] += 16
        inst.then_inc(self.dsems[di], 16)
        self.ninst += 1
        self._record((self.dsems[di], self.dcnt[di]), reads, writes)
        if sync:
            self.eng[ek].wait_ge(self.dsems[di], self.dcnt[di])
            self.known[ek][self.dsems[di]] = self.dcnt[di]

    def barrier(self, engs=None):
        for ek in (engs or self.ENG):
            eng = self.eng[ek]
            kn = self.known[ek]
            toks = [self.last[k2] for k2 in self.ENG if k2 != ek and self.last[k2] is not None]
            toks += self.dall
            toks += [(s, v) for s, v in zip(self.dsems, self.dcnt) if v]
            for s, v in toks:
                if kn.get(s, 0) < v:
                    eng.wait_ge(s, v)
                    kn[s] = v


class Prog:
    def __init__(self, nc, es):
        self.nc = nc
        self.es = es
        self.fw = FW(nc, es)
        fw = self.fw
        nc_ = nc
        self.pb = [es.enter_context(nc.psum_tensor("pb%d" % i, [128, 512], F32)) for i in range(8)]
        self.pbi = 0
        self.ident = self.sb(es, "ident", [128, 128], F32)
        self.iota16 = self.sb(es, "iota16", [128, 16], F32)
        self.ones_bf = self.sb(es, "ones_bf", [128, 128], BF16)
        self.identb = self.sb(es, "identb", [128, 128], BF16)
        self.eps_ln = self.sb(es, "eps_ln", [128, 1], F32)
        self.eps_rms = self.sb(es, "eps_rms", [128, 1], F32)
        iot = self.sb(es, "iot", [128, 128], F32)
        pidx = self.sb(es, "pidx", [128, 1], F32)
        G = nc.gpsimd
        fw.op("pool", lambda: G.iota(iot[:], [[1, 128]], base=0, channel_multiplier=0,
                                     allow_small_or_imprecise_dtypes=True), writes=["iot"])
        fw.op("pool", lambda: G.iota(pidx[:], [[1, 1]], base=0, channel_multiplier=1,
                                     allow_small_or_imprecise_dtypes=True), writes=["pidx"])
        fw.op("pool", lambda: G.iota(self.iota16[:], [[1, 16]], base=0, channel_multiplier=0,
                                     allow_small_or_imprecise_dtypes=True), writes=["iota16"])
        fw.op("dve", lambda: nc.vector.tensor_scalar(self.ident[:], iot[:], pidx[:, 0:1], None, ALU.is_equal),
              reads=["iot", "pidx"], writes=["ident"])
        fw.op("dve", lambda: nc.vector.memset(self.ones_bf[:], 1.0), writes=["ones_bf"])
        fw.op("dve", lambda: nc.vector.tensor_copy(self.identb[:], self.ident[:]), reads=["ident"], writes=["identb"])
        fw.op("dve", lambda: nc.vector.memset(self.eps_ln[:], LN_EPS), writes=["eps"])
        fw.op("dve", lambda: nc.vector.memset(self.eps_rms[:], RMS_EPS), writes=["eps"])
        self.ds = {}

    def sb(self, es, name, shape, dtype):
        self.nsb = getattr(self, "nsb", 0) + 1
        return es.enter_context(self.nc.sbuf_tensor("sb%d_%s" % (self.nsb, name), list(shape), dtype))

    def dsem(self, name):
        if name not in self.ds:
            self.ds[name] = self.fw.dsem()
        return self.ds[name]

    def next_ps(self, lo=0, hi=None):
        if hi is None:
            hi = getattr(self, "ps_hi", 8)
        i = lo + self.pbi % (hi - lo)
        self.pbi += 1
        return self.pb[i], "pb%d" % i

    def transpose_bf(self, src, srckey, dst, dstkey, nchunk, rows=128):
        nc, fw = self.nc, self.fw
        for c0 in range(0, nchunk, 4):
            n = min(4, nchunk - c0)
            pt, pn = self.next_ps()
            for j in range(n):
                c = c0 + j
                fw.op("pe", lambda: nc.tensor.transpose(pt[:, j * rows:(j + 1) * rows], src[0:rows, c * 128:(c + 1) * 128],
                                                        self.ident[0:rows, 0:rows]),
                      reads=[srckey, "ident"], writes=[pn], same_ok=True)
            fw.op("act", lambda: nc.scalar.copy(dst[:, c0:c0 + n, :], pt[:, 0:n * rows].rearrange("p (a b) -> p a b", a=n)),
                  reads=[pn], writes=[dstkey])

    def modulate(self, xt, xkey, mod, shift_off, scale_off, out, outkey, rows=128):
        nc, fw = self.nc, self.fw
        V = nc.vector
        fw.op("dve", lambda: V.tensor_tensor(out=out[0:rows, :], in0=xt[0:rows, :], in1=mod[0:rows, scale_off:scale_off + D], op=ALU.mult),
              reads=[xkey, "mod"], writes=[outkey])
        fw.op("pool", lambda: nc.gpsimd.tensor_tensor(out=out[0:rows, :], in0=out[0:rows, :], in1=mod[0:rows, shift_off:shift_off + D], op=ALU.add),
              reads=[outkey, "mod"], writes=[outkey])

    def post_norm(self, xt, xkey, y_parts, ykeys, mod, gate_off, lng, lnb, tmp, tmpkey, out, outkey, sm):
        nc, fw = self.nc, self.fw
        V = nc.vector
        for i, yp in enumerate(y_parts):
            fw.op("dve", lambda: V.tensor_tensor(out=tmp[:, i * 512:(i + 1) * 512], in0=yp, in1=mod[:, gate_off + i * 512:gate_off + (i + 1) * 512], op=ALU.mult),
                  reads=[ykeys[i], "mod"], writes=[tmpkey])
        fw.op("dve", lambda: V.scalar_tensor_tensor(out=tmp[:], in0=xt[:], scalar=ALPHA, in1=tmp[:], op0=ALU.mult, op1=ALU.add),
              reads=[xkey, tmpkey], writes=[tmpkey])
        stats, mv, sd = sm
        for i in range(2):
            fw.op("dve", lambda: V.bn_stats(out=stats[:, i, :], in_=tmp[:, i * 512:(i + 1) * 512]), reads=[tmpkey], writes=["ln_stats%d" % i])
        fw.op("dve", lambda: V.bn_aggr(out=mv[:], in_=stats[:].rearrange("p a b -> p (a b)")), reads=["ln_stats0", "ln_stats1"], writes=["ln_mv"])
        fw.op("act", lambda: nc.scalar.activation(out=sd[:], in_=mv[:, 1:2], func=AF.Sqrt, bias=self.eps_ln[:, 0:1]), reads=["ln_mv", "eps"], writes=["ln_sd"])
        fw.op("dve", lambda: V.reciprocal(out=sd[:], in_=sd[:]), reads=["ln_sd"], writes=["ln_sd"])
        fw.op("dve", lambda: V.tensor_scalar(tmp[:], tmp[:], mv[:, 0:1], sd[:, 0:1], ALU.subtract, ALU.mult),
              reads=[tmpkey, "ln_mv", "ln_sd"], writes=[tmpkey])
        fw.op("pool", lambda: nc.gpsimd.tensor_tensor(out=tmp[:], in0=tmp[:], in1=lng, op=ALU.mult), reads=[tmpkey, "lnp"], writes=[tmpkey])
        fw.op("dve", lambda: V.tensor_tensor(out=out[:], in0=tmp[:], in1=lnb, op=ALU.add), reads=[tmpkey, "lnp"], writes=[outkey])

    def load_mod(self, mod, modrows, which):
        nc, fw = self.nc, self.fw
        fw.dma("sp", lambda: nc.sync.dma_start(out=mod[:], in_=modrows[which:which + 1, :].broadcast_to([128, 6 * D])),
               self.dsem("mod"), reads=["modrows"], writes=["mod"])

    def phase_mod(self, a, modrows):
        nc, fw = self.nc, self.fw
        V = nc.vector
        with ExitStack() as es:
            ccol = self.sb(es, "ccol", [128, 16], F32)
            scol = self.sb(es, "scol", [128, 16], F32)
            brow = self.sb(es, "brow", [1, 6 * D], F32)
            mrow = self.sb(es, "mrow", [1, 2, 6 * D], F32)
            wch = [self.sb(es, "wch%d" % i, [128, 8, 512], F32) for i in range(2)]
            fw.dma("sp", lambda: nc.sync.dma_start(out=ccol[:], in_=a["ccol"][:, :]), self.dsem("misc"), sync=True, writes=["ccol"])
            fw.dma("sp", lambda: nc.sync.dma_start(out=brow[:], in_=a["bmod"][:, :]), self.dsem("misc"), sync=True, writes=["brow"])
            fw.op("act", lambda: nc.scalar.activation(out=scol[:], in_=ccol[:], func=AF.Silu), reads=["ccol"], writes=["scol"])
            wv = a["wmod"].rearrange("(k p) n -> p k n", p=128)
            for n in range(12):
                w = wch[n % 2]
                wk = "wch%d" % (n % 2)
                fw.dma("sp", lambda: nc.sync.dma_start(out=w[:], in_=wv[:, :, n * 512:(n + 1) * 512]), self.dsem(wk), writes=[wk])
                plus1 = 1.0 if n in (2, 3, 8, 9) else 0.0
                for which in range(2):
                    pt, pn = self.next_ps()
                    for k in range(8):
                        fw.op("pe", lambda: nc.tensor.matmul(pt[0:1, :], lhsT=scol[:, which * 8 + k:which * 8 + k + 1], rhs=w[:, k, :],
                                                             start=(k == 0), stop=(k == 7)),
                              reads=["scol", wk], writes=[pn], same_ok=True)
                    fw.op("dve", lambda: V.scalar_tensor_tensor(out=mrow[0:1, which, n * 512:(n + 1) * 512], in0=pt[0:1, :], scalar=plus1,
                                                                in1=brow[0:1, n * 512:(n + 1) * 512], op0=ALU.add, op1=ALU.add),
                          reads=[pn, "brow"], writes=["mrow"])
            fw.dma("sp", lambda: nc.sync.dma_start(out=modrows[:, :], in_=mrow[0:1, :, :].rearrange("p a n -> p (a n)")),
                   self.dsem("misc"), sync=True, reads=["mrow"], writes=["modrows"])
            fw.barrier()

    def phase_cvt(self, a, uv_bf):
        nc, fw = self.nc, self.fw
        with ExitStack() as es:
            st = [self.sb(es, "cvt%d" % i, [128, 4, D], BF16) for i in range(4)]
            i = 0
            for (src, dst) in ((a["pu"], uv_bf[:, 0:D]), (a["pv"], uv_bf[:, D:2 * D])):
                for c in range(16384 // 512):
                    t = st[i % 4]
                    k = "cvt%d" % (i % 4)
                    fw.dma("pool", lambda: nc.gpsimd.dma_start(out=t[:], in_=src[c * 512:(c + 1) * 512, :].rearrange("(k p) d -> p k d", p=128)),
                           self.dsem(k + "l"), writes=[k])
                    fw.dma("sp", lambda: nc.sync.dma_start(out=dst[c * 512:(c + 1) * 512, :].rearrange("(k p) d -> p k d", p=128), in_=t[:]),
                           self.dsem(k + "s"), reads=[k], writes=["tabbf"])
                    i += 1
            fw.barrier()

    def phase_peer(self, a, modrows, x1s, n_x, dst_fn, ctx_needed, uv_bf):
        nc, fw = self.nc, self.fw
        V = nc.vector
        with ExitStack() as es:
            mod = self.sb(es, "mod", [128, 3 * D], F32)
            lng = self.sb(es, "lng", [128, D], F32)
            lnb = self.sb(es, "lnb", [128, D], F32)
            wq = self.sb(es, "wq_bf", [128, 8, 2048], BF16)
            kT = self.sb(es, "kT", [128, 16, 128], BF16)
            xts = [self.sb(es, "p_xt%d" % i, [128, D], F32) for i in range(2)]
            h2 = self.sb(es, "p_h2", [128, D], F32)
            tmp = h2
            h2bs = [self.sb(es, "p_h2b%d" % i, [128, D], BF16) for i in range(2)]
            xo = self.sb(es, "p_xo", [128, D], F32)
            h2T = self.sb(es, "p_h2T", [128, 8, 128], BF16)
            qT = self.sb(es, "p_qT", [128, 16, 128], BF16)
            arena = self.sb(es, "p_arena", [128, 4, 2048], F32)
            s_sb = arena[:, 0, :].rearrange("p (g n) -> p g n", g=16)
            s_wk = arena[:, 1, :].rearrange("p (g n) -> p g n", g=16)
            cand = arena[:, 2, :].rearrange("p (h n) -> p h n", h=8)
            cand_wk = arena[:, 3, :].rearrange("p (h n) -> p h n", h=8)
            oh = arena[:, 0, :].rearrange("p (h k a) -> p h k a", h=8, k=16)
            oh2 = arena[:, 1, :].rearrange("p (h k a) -> p h k a", h=8, k=16)
            m1 = self.sb(es, "p_m1", [128, 16, 16], F32)
            i1 = self.sb(es, "p_i1", [128, 16, 16], U32)
            i1f = self.sb(es, "p_i1f", [128, 16, 16], F32)
            m2 = self.sb(es, "p_m2", [128, 8, 16], F32)
            p2 = self.sb(es, "p_p2", [128, 8, 16], U32)
            pa = self.sb(es, "p_pa", [128, 8, 16], U32)
            pbb = self.sb(es, "p_pb", [128, 8, 16], U32)
            paf = self.sb(es, "p_paf", [128, 8, 16], F32)
            pbf = self.sb(es, "p_pbf", [128, 8, 16], F32)
            sel1 = self.sb(es, "p_sel1", [128, 8, 16], F32)
            sel2 = self.sb(es, "p_sel2", [128, 8, 16], F32)
            idxf = self.sb(es, "p_idxf", [128, 128], F32)
            idxus = [self.sb(es, "p_idxu%d" % i, [128, 128], U32) for i in range(2)]
            ee = self.sb(es, "p_ee", [128, 8, 16], F32)
            zz = self.sb(es, "p_zz", [128, 8], F32)
            ggs = [self.sb(es, "p_gg%d" % i, [128, 8, 16], F32) for i in range(2)]
            apre = self.sb(es, "p_apre", [128, 128], F32)
            wgt = self.sb(es, "p_wgt", [128, 128], F32)
            prod = [self.sb(es, "p_prod%d" % i, [128, D], BF16) for i in range(4)]
            with ExitStack() as es_k:
                kraw = self.sb(es_k, "kraw", [128, 16, 128], F32)
                fw.dma("sp", lambda: nc.sync.dma_start(out=kraw[:], in_=a["kk"].rearrange("g n c -> n g c")), self.dsem("misc"), sync=True, writes=["kraw"])
                for g4 in range(4):
                    pt, pn = self.next_ps()
                    for gg in range(4):
                        g = g4 * 4 + gg
                        fw.op("pe", lambda: nc.tensor.transpose(pt[:, gg * 128:(gg + 1) * 128], kraw[:, g, :], self.ident[:]),
                              reads=["kraw", "ident"], writes=[pn], same_ok=True)
                    fw.op("act", lambda: nc.scalar.copy(kT[:, g4 * 4:(g4 + 1) * 4, :], pt[:].rearrange("p (a b) -> p a b", a=4)),
                          reads=[pn], writes=["kT"])

                fw.barrier()
            NSLOT = 16
            GJ = 2
            cb = [self.sb(es, "p_cb%d" % i, [128, 2 * D], BF16) for i in range(NSLOT)]
            dgs = [self.sb(es, "p_dg%d" % i, [128, GJ, 128], BF16) for i in range(2)]
            ge = self.sb(es, "p_ge", [128, 128], F32)
            cbs = [self.dsem("cb%d" % i) for i in range(NSLOT)]
            self.ps_hi = 6
            accp = [(self.pb[6], "pb6"), (self.pb[7], "pb7")]
            stats = self.sb(es, "p_stats", [128, 2, 6], F32)
            mv = self.sb(es, "p_mv", [128, 2], F32)
            sd = self.sb(es, "p_sd", [128, 1], F32)

            fw.dma("pool", lambda: nc.gpsimd.dma_start(out=wq[:], in_=a["wq"].rearrange("(k p) n -> p k n", p=128)),
                   self.dsem("wload"), sync=True, writes=["wq"])
            fw.dma("sp", lambda: nc.sync.dma_start(out=lng[:], in_=a["lng"][1:2, :].broadcast_to([128, D])), self.dsem("misc"), sync=True, writes=["lnp"])
            fw.dma("sp", lambda: nc.sync.dma_start(out=lnb[:], in_=a["lnb"][1:2, :].broadcast_to([128, D])), self.dsem("misc"), sync=True, writes=["lnp"])
            def load_mod3(which):
                fw.dma("sp", lambda: nc.sync.dma_start(out=mod[:], in_=modrows[which:which + 1, 3 * D:6 * D].broadcast_to([128, 3 * D])),
                       self.dsem("mod"), reads=["modrows"], writes=["mod"])

            SSB = ["s_sb%d" % g for g in range(4)]
            SWK = ["s_wk%d" % g for g in range(16)]
            n_tiles = n_x + (2 if ctx_needed else 0)
            ggfs = [g_[:].rearrange("p h k -> p (h k)") for g_ in ggs]

            def front(t):
                b = t % 2
                xt, h2b, idxu, gg_ = xts[b], h2bs[b], idxus[b], ggs[b]
                xk, hk, ik, gk = "p_xt%d" % b, "h2b_%d" % b, "idxu%d" % b, "gg%d" % b
                src = x1s[t * 128:(t + 1) * 128, :]
                fw.dma("sp", lambda: nc.sync.dma_start(out=xt[:], in_=src), self.dsem(xk), reads=["x1s"], writes=[xk])
                yield
                self.modulate(xt, xk, mod, 0, D, h2, "h2")
                fw.op("act", lambda: nc.scalar.copy(h2b[:], h2[:]), reads=["h2"], writes=[hk])
                yield
                self.transpose_bf(h2, "h2", h2T, "h2T", 8)
                yield
                for g4 in range(4):
                    pt, pn = self.next_ps()
                    for gg in range(4):
                        fc = g4 * 4 + gg
                        for k in range(8):
                            fw.op("pe", lambda: nc.tensor.matmul(pt[:, gg * 128:(gg + 1) * 128], lhsT=wq[:, k, fc * 128:(fc + 1) * 128],
                                                                 rhs=h2T[:, k, :], start=(k == 0), stop=(k == 7)),
                                  reads=["wq", "h2T"], writes=[pn], same_ok=True)
                        yield
                    if g4 % 2 == 0:
                        fw.op("act", lambda: nc.scalar.copy(qT[:, g4 * 4:(g4 + 1) * 4, :], pt[:].rearrange("p (a b) -> p a b", a=4)),
                              reads=[pn], writes=["qT%d" % g4])
                    else:
                        fw.op("dve", lambda: V.tensor_copy(qT[:, g4 * 4:(g4 + 1) * 4, :], pt[:].rearrange("p (a b) -> p a b", a=4)),
                              reads=[pn], writes=["qT%d" % g4])
                    yield
                for g4 in range(4):
                    pt, pn = self.next_ps()
                    for gg in range(4):
                        fc = g4 * 4 + gg
                        fw.op("pe", lambda: nc.tensor.matmul(pt[:, gg * 128:(gg + 1) * 128], lhsT=qT[:, fc, :], rhs=kT[:, fc, :], start=True, stop=True),
                              reads=["qT%d" % g4, "kT"], writes=[pn], same_ok=True)
                    fw.op("act", lambda: nc.scalar.copy(s_sb[:, g4 * 4:(g4 + 1) * 4, :], pt[:].rearrange("p (a b) -> p a b", a=4)),
                          reads=[pn], writes=["s_sb%d" % g4, "oh"])
                    yield
                for g in range(16):
                    fw.op("dve", lambda: V.max(out=m1[:, g, 0:8], in_=s_sb[:, g, :]), reads=["s_sb%d" % (g // 4)], writes=["m1a%d" % g])
                    yield
                for g in range(16):
                    fw.op("dve", lambda: V.max_index(out=i1[:, g, 0:8], in_max=m1[:, g, 0:8], in_values=s_sb[:, g, :]),
                          reads=["s_sb%d" % (g // 4), "m1a%d" % g], writes=["i1a%d" % g])
                    yield
                for g in range(16):
                    fw.op("dve", lambda: V.match_replace(out=s_wk[:, g, :], in_to_replace=m1[:, g, 0:8], in_values=s_sb[:, g, :], imm_value=-1e30),
                          reads=["s_sb%d" % (g // 4), "m1a%d" % g, "oh2"], writes=["s_wk%d" % g])
                    yield
                for g in range(16):
                    fw.op("dve", lambda: V.max(out=m1[:, g, 8:16], in_=s_wk[:, g, :]), reads=["s_wk%d" % g], writes=["m1b%d" % g])
                    yield
                for g in range(16):
                    fw.op("dve", lambda: V.max_index(out=i1[:, g, 8:16], in_max=m1[:, g, 8:16], in_values=s_wk[:, g, :]),
                          reads=["s_wk%d" % g, "m1b%d" % g], writes=["i1b%d" % g])
                    yield
                allm1 = ["m1a%d" % g for g in range(16)] + ["m1b%d" % g for g in range(16)]
                alli1 = ["i1a%d" % g for g in range(16)] + ["i1b%d" % g for g in range(16)]
                m1v = m1[:].rearrange("p (h j) a -> p h j a", j=2)
                fw.op("dve", lambda: V.tensor_tensor(out=cand.rearrange("p h (a b) -> p h a b", a=16),
                                                     in0=m1v[:, :, 0, :].unsqueeze(3).broadcast_to([128, 8, 16, 16]),
                                                     in1=m1v[:, :, 1, :].unsqueeze(2).broadcast_to([128, 8, 16, 16]), op=ALU.add),
                      reads=allm1, writes=["cand"])
                yield
                for h in range(8):
                    fw.op("dve", lambda: V.max(out=m2[:, h, 0:8], in_=cand[:, h, :]), reads=["cand"], writes=["m2a%d" % h])
                    yield
                for h in range(8):
                    fw.op("dve", lambda: V.max_index(out=p2[:, h, 0:8], in_max=m2[:, h, 0:8], in_values=cand[:, h, :]),
                          reads=["cand", "m2a%d" % h], writes=["p2a%d" % h])
                    yield
                for h in range(8):
                    fw.op("dve", lambda: V.match_replace(out=cand_wk[:, h, :], in_to_replace=m2[:, h, 0:8], in_values=cand[:, h, :], imm_value=-1e30),
                          reads=["cand", "m2a%d" % h], writes=["cwk%d" % h])
                    yield
                for h in range(8):
                    fw.op("dve", lambda: V.max(out=m2[:, h, 8:16], in_=cand_wk[:, h, :]), reads=["cwk%d" % h], writes=["m2b%d" % h])
                    yield
                for h in range(8):
                    fw.op("dve", lambda: V.max_index(out=p2[:, h, 8:16], in_max=m2[:, h, 8:16], in_values=cand_wk[:, h, :]),
                          reads=["cwk%d" % h, "m2b%d" % h], writes=["p2b%d" % h])
                    yield
                allm2 = ["m2a%d" % h for h in range(8)] + ["m2b%d" % h for h in range(8)]
                allp2 = ["p2a%d" % h for h in range(8)] + ["p2b%d" % h for h in range(8)]
                fw.op("dve", lambda: V.tensor_tensor(out=ee[:], in0=m2[:], in1=m2[:, :, 0:1].broadcast_to([128, 8, 16]), op=ALU.subtract),
                      reads=allm2, writes=["ee"])
                fw.op("act", lambda: nc.scalar.activation(out=ee[:], in_=ee[:], func=AF.Exp), reads=["ee"], writes=["ee"])
                yield
                fw.op("dve", lambda: V.reduce_sum(out=zz[:], in_=ee[:], axis=AX.X), reads=["ee"], writes=["zz"])
                fw.op("dve", lambda: V.reciprocal(out=zz[:], in_=zz[:]), reads=["zz"], writes=["zz"])
                fw.op("dve", lambda: V.tensor_tensor(out=gg_[:], in0=ee[:], in1=zz[:].unsqueeze(2).broadcast_to([128, 8, 16]), op=ALU.mult),
                      reads=["ee", "zz"], writes=[gk])
                yield
                fw.op("dve", lambda: V.tensor_single_scalar(out=pa[:], in_=p2[:], scalar=4, op=ALU.logical_shift_right), reads=allp2, writes=["pa"])
                fw.op("dve", lambda: V.tensor_single_scalar(out=pbb[:], in_=p2[:], scalar=15, op=ALU.bitwise_and), reads=allp2, writes=["pbb"])
                yield
                fw.op("dve", lambda: V.tensor_copy(paf[:], pa[:]), reads=["pa"], writes=["paf"])
                fw.op("dve", lambda: V.tensor_copy(pbf[:], pbb[:]), reads=["pbb"], writes=["pbf"])
                fw.op("dve", lambda: V.tensor_copy(i1f[:], i1[:]), reads=alli1, writes=["i1f"])
                yield
                i1v = i1f[:].rearrange("p (h j) a -> p h j a", j=2)
                io_b = self.iota16[:].unsqueeze(1).unsqueeze(1).broadcast_to([128, 8, 16, 16])
                for (pf, pfn, jj, sel, seln, ohh, ohn, alias) in ((paf, "paf", 0, sel1, "sel1", oh, "oh", SSB), (pbf, "pbf", 1, sel2, "sel2", oh2, "oh2", SWK)):
                    fw.op("dve", lambda: V.tensor_tensor(out=ohh, in0=io_b, in1=pf[:].unsqueeze(3).broadcast_to([128, 8, 16, 16]), op=ALU.is_equal),
                          reads=["iota16", pfn], writes=[ohn] + alias)
                    yield
                    fw.op("dve", lambda: V.tensor_tensor(out=ohh, in0=ohh, in1=i1v[:, :, jj, :].unsqueeze(2).broadcast_to([128, 8, 16, 16]), op=ALU.mult),
                          reads=[ohn, "i1f"], writes=[ohn] + alias)
                    yield
                    fw.op("dve", lambda: V.reduce_sum(out=sel[:], in_=ohh, axis=AX.X), reads=[ohn], writes=[seln])
                    yield
                fw.op("dve", lambda: V.scalar_tensor_tensor(out=idxf[:], in0=sel1[:].rearrange("p h k -> p (h k)"), scalar=128.0,
                                                            in1=sel2[:].rearrange("p h k -> p (h k)"), op0=ALU.mult, op1=ALU.add),
                      reads=["sel1", "sel2"], writes=["idxf"])
                fw.op("dve", lambda: V.tensor_copy(idxu[:], idxf[:]), reads=["idxf"], writes=[ik])
                yield

            def drain(gen, n=None):
                if gen is None:
                    return
                k = 0
                for _ in gen:
                    k += 1
                    if n is not None and k >= n:
                        return

            load_mod3(0)
            drain(front(0))
            for t in range(n_tiles):
                b = t % 2
                xt, h2b, idxu = xts[b], h2bs[b], idxus[b]
                xk, hk, ik, gk = "p_xt%d" % b, "h2b_%d" % b, "idxu%d" % b, "gg%d" % b
                ggf = ggfs[b]
                dst, okey = dst_fn(t)
                nxt = front(t + 1) if (t + 1 < n_tiles and t + 1 != n_x) else None
                def stage_a(gi):
                    for jj in range(GJ):
                        j = gi * GJ + jj
                        r = j % NSLOT
                        pr = j % 4
                        fw.dma("pool", lambda: nc.gpsimd.indirect_dma_start(out=cb[r][:], out_offset=None, in_=uv_bf[:, :],
                                                                            in_offset=bass.IndirectOffsetOnAxis(ap=idxu[:, j:j + 1], axis=0)),
                               cbs[r], reads=[ik], writes=["cb%d" % r])
                        fw.op("dve", lambda: V.tensor_tensor(out=prod[pr][:], in0=cb[r][:, 0:D], in1=h2b[:], op=ALU.mult),
                              reads=["cb%d" % r, hk], writes=["prod%d" % pr])
                        fw.op("act", lambda: nc.scalar.activation(out=prod[pr][:], in_=prod[pr][:], func=AF.Copy, accum_out=apre[:, j:j + 1]),
                              reads=["prod%d" % pr], writes=["prod%d" % pr, "apre%d" % (gi % 4)], same_ok=True)
                    gs = slice(gi * GJ, (gi + 1) * GJ)
                    fw.op("act", lambda: nc.scalar.activation(out=ge[:, gs], in_=apre[:, gs], func=AF.Gelu), reads=["apre%d" % (gi % 4)], writes=["ge%d" % (gi % 4)])

                def stage_b(gi):
                    gs = slice(gi * GJ, (gi + 1) * GJ)
                    fw.op("dve", lambda: V.tensor_tensor(out=wgt[:, gs], in0=ge[:, gs], in1=ggf[:, gs], op=ALU.mult),
                          reads=["ge%d" % (gi % 4), gk], writes=["wgt%d" % (gi % 4)])
                    dg = dgs[gi % 2]
                    dk = "dg%d" % (gi % 2)
                    fw.op("dve", lambda: V.tensor_tensor(out=dg[:], in0=wgt[:, gs].unsqueeze(2).broadcast_to([128, GJ, 128]),
                                                         in1=self.identb[:].unsqueeze(1).broadcast_to([128, GJ, 128]), op=ALU.mult),
                          reads=["wgt%d" % (gi % 4), "identb"], writes=[dk])
                    for jj in range(GJ):
                        j = gi * GJ + jj
                        r = j % NSLOT
                        for n in range(2):
                            fw.op("pe", lambda: nc.tensor.matmul(accp[n][0][:, :], lhsT=dg[:, jj, :], rhs=cb[r][:, D + n * 512:D + (n + 1) * 512],
                                                                 start=(j == 0), stop=(j == 127)),
                                  reads=[dk, "cb%d" % r], writes=[accp[n][1]], same_ok=True)

                NG = 128 // GJ
                for gi in range(NG):
                    stage_a(gi)
                    if gi >= 1:
                        stage_b(gi - 1)
                    drain(nxt, 4)
                stage_b(NG - 1)
                drain(nxt)
                self.post_norm(xt, xk, [accp[0][0][:, :], accp[1][0][:, :]], ["pb6", "pb7"], mod, 2 * D, lng[:], lnb[:],
                               tmp, "h2", xo, "p_xo", (stats, mv, sd))
                fw.dma("sp", lambda: nc.sync.dma_start(out=dst, in_=xo[:]), self.dsem("xo"), reads=["p_xo"], writes=[okey])
                if t + 1 < n_tiles and t + 1 == n_x:
                    load_mod3(1)
                    drain(front(t + 1))
            self.ps_hi = 8
            fw.barrier()

    def phase_pool(self, a, modrows, load_x, n_x, halo, ctx, x1s, ctx_needed):
        nc, fw = self.nc, self.fw
        V = nc.vector
        with ExitStack() as es:
            mod = self.sb(es, "mod", [128, 6 * D], F32)
            lng = self.sb(es, "lng", [128, D], F32)
            lnb = self.sb(es, "lnb", [128, D], F32)
            w_in = self.sb(es, "pw_in", [128, 8, D], BF16)
            w_grp = self.sb(es, "pw_grp", [128, 8, 256], BF16)
            w_out = self.sb(es, "pw_out", [128, 8, D], BF16)
            psc = self.sb(es, "pscale", [128, 8], F32)
            band = self.sb(es, "band", [128, 4, 9, 128], F32)
            zb = [self.sb(es, "zb%d" % i, [128, D], F32) for i in range(4)]
            zh = self.sb(es, "zh", [16, D], F32)
            xts = [self.sb(es, "m_xt%d" % i, [128, D], F32) for i in range(2)]
            hh = self.sb(es, "m_h", [128, D], F32)
            hT = self.sb(es, "m_hT", [128, 8, 128], BF16)
            hT16 = self.sb(es, "m_hT16", [128, 8, 16], BF16)
            pooledT = self.sb(es, "pooledT", [128, 8, 128], BF16)
            y1T = self.sb(es, "y1T", [128, 8, 128], BF16)
            tmp = self.sb(es, "m_tmp", [128, D], F32)
            xo = self.sb(es, "m_xo", [128, D], F32)
            stats = self.sb(es, "m_stats", [128, 2, 6], F32)
            mv = self.sb(es, "m_mv", [128, 2], F32)
            sd = self.sb(es, "m_sd", [128, 1], F32)
            G = nc.gpsimd
            fw.dma("pool", lambda: G.dma_start(out=w_in[:], in_=a["pw_in"].rearrange("(k p) n -> p k n", p=128)), self.dsem("wload"), sync=True, writes=["pw_in"])
            fw.dma("pool", lambda: G.dma_start(out=w_out[:], in_=a["pw_out"].rearrange("(k p) n -> p k n", p=128)), self.dsem("wload"), sync=True, writes=["pw_out"])
            fw.dma("pool", lambda: G.dma_start(out=w_grp[:].rearrange("p (g c) n -> p g c n", g=4), in_=a["pw_grp"].rearrange("g (c p) n -> p g c n", p=128)),
                   self.dsem("wload"), sync=True, writes=["pw_grp"])
            fw.dma("sp", lambda: nc.sync.dma_start(out=psc[:], in_=a["pscale"][:, :]), self.dsem("misc"), sync=True, writes=["psc"])
            fw.dma("sp", lambda: nc.sync.dma_start(out=band[:], in_=a["band"][:, :, :, :]), self.dsem("misc"), sync=True, writes=["band"])
            fw.dma("sp", lambda: nc.sync.dma_start(out=lng[:], in_=a["lng"][0:1, :].broadcast_to([128, D])), self.dsem("misc"), sync=True, writes=["lnp"])
            fw.dma("sp", lambda: nc.sync.dma_start(out=lnb[:], in_=a["lnb"][0:1, :].broadcast_to([128, D])), self.dsem("misc"), sync=True, writes=["lnp"])

            def compute_z(loader, zt, zkey, xt, xkey, rows=128):
                loader(xt, xkey)
                self.modulate(xt, xkey, mod, 0, D, hh, "m_h", rows=rows)
                if rows == 128:
                    self.transpose_bf(hh, "m_h", hT, "m_hT", 8)
                    lt = hT
                    lk = "m_hT"
                else:
                    self.transpose_bf(hh, "m_h", hT16, "m_hT16", 8, rows=rows)
                    lt = hT16
                    lk = "m_hT16"
                for n in range(2):
                    pt, pn = self.next_ps()
                    for k in range(8):
                        fw.op("pe", lambda: nc.tensor.matmul(pt[0:rows, :], lhsT=lt[:, k, :], rhs=w_in[:, k, n * 512:(n + 1) * 512], start=(k == 0), stop=(k == 7)),
                              reads=[lk, "pw_in"], writes=[pn], same_ok=True)
                    fw.op("act", lambda: nc.scalar.copy(zt[0:rows, n * 512:(n + 1) * 512], pt[0:rows, :]), reads=[pn], writes=[zkey])

            def mix_tile(zprev, zcur, znext, vcen, zhv, xt, xkey, dst):
                for half in range(2):
                    pt, pn = self.next_ps()
                    for j in range(4):
                        cc = half * 4 + j
                        g = cc // 2
                        terms = []
                        if zprev is not None:
                            terms.append((zprev[0][:, cc * 128:(cc + 1) * 128], band[:, g, 1, :], zprev[1]))
                        terms.append((zcur[0][:, cc * 128:(cc + 1) * 128], band[:, g, vcen, :], zcur[1]))
                        if znext is not None:
                            terms.append((znext[0][:, cc * 128:(cc + 1) * 128], band[:, g, 2, :], znext[1]))
                        if zhv is not None:
                            terms.append((zh[0:16, cc * 128:(cc + 1) * 128], band[0:16, g, zhv, :], "zh"))
                        for ti, (l, r, lk) in enumerate(terms):
                            fw.op("pe", lambda: nc.tensor.matmul(pt[:, j * 128:(j + 1) * 128], lhsT=l, rhs=r, start=(ti == 0), stop=(ti == len(terms) - 1)),
                                  reads=[lk, "band"], writes=[pn], same_ok=True)
                    fw.op("act", lambda: nc.scalar.copy(pooledT[:, half * 4:(half + 1) * 4, :], pt[:].rearrange("p (a b) -> p a b", a=4)),
                          reads=[pn], writes=["pooledT"])
                for half in range(2):
                    pt, pn = self.next_ps()
                    for j in range(4):
                        dc = half * 4 + j
                        g = dc // 2
                        m = dc % 2
                        for c in range(2):
                            fw.op("pe", lambda: nc.tensor.matmul(pt[:, j * 128:(j + 1) * 128], lhsT=w_grp[:, g * 2 + c, m * 128:(m + 1) * 128],
                                                                 rhs=pooledT[:, g * 2 + c, :], start=(c == 0), stop=(c == 1)),
                                  reads=["pw_grp", "pooledT"], writes=[pn], same_ok=True)
                    fw.op("dve", lambda: V.tensor_tensor(out=y1T[:, half * 4:(half + 1) * 4, :], in0=pt[:].rearrange("p (a b) -> p a b", a=4),
                                                         in1=psc[:, half * 4:(half + 1) * 4].unsqueeze(2).broadcast_to([128, 4, 128]), op=ALU.mult),
                          reads=[pn, "psc"], writes=["y1T"])
                yp = []
                for n in range(2):
                    pt, pn = self.next_ps()
                    for k in range(8):
                        fw.op("pe", lambda: nc.tensor.matmul(pt[:, :], lhsT=y1T[:, k, :], rhs=w_out[:, k, n * 512:(n + 1) * 512], start=(k == 0), stop=(k == 7)),
                              reads=["y1T", "pw_out"], writes=[pn], same_ok=True)
                    yp.append((pt, pn))
                self.post_norm(xt, xkey, [yp[0][0][:, :], yp[1][0][:, :]], [yp[0][1], yp[1][1]], mod, 2 * D, lng[:], lnb[:],
                               tmp, "m_tmp", xo, "m_xo", (stats, mv, sd))
                fw.dma("sp", lambda: nc.sync.dma_start(out=dst, in_=xo[:]), self.dsem("xo"), reads=["m_xo"], writes=["x1s"])

            def plain(src, srckey, rows=128):
                return lambda xt, xkey: fw.dma("sp", lambda: nc.sync.dma_start(out=xt[0:rows, :], in_=src), self.dsem(xkey), reads=[srckey], writes=[xkey])

            self.load_mod(mod, modrows, 0)
            if halo is not None:
                compute_z(plain(halo, "xsrc", 16), zh, "zh", xts[0], "m_xt0", rows=16)
            compute_z(lambda xt, xkey: load_x(0, xt, xkey), zb[0], "zb0", xts[0], "m_xt0")
            for T in range(n_x):
                if T + 1 < n_x:
                    compute_z(lambda xt, xkey: load_x(T + 1, xt, xkey), zb[(T + 1) % 4], "zb%d" % ((T + 1) % 4), xts[(T + 1) % 2], "m_xt%d" % ((T + 1) % 2))
                zprev = (zb[(T - 1) % 4], "zb%d" % ((T - 1) % 4)) if T > 0 else None
                znext = (zb[(T + 1) % 4], "zb%d" % ((T + 1) % 4)) if T + 1 < n_x else None
                vcen = 3 if T == 0 else (4 if T == n_x - 1 else 0)
                zhv = None
                if halo is not None:
                    zhv = 7 if T == 0 else (8 if T == n_x - 1 else None)
                mix_tile(zprev, (zb[T % 4], "zb%d" % (T % 4)), znext, vcen, zhv, xts[T % 2], "m_xt%d" % (T % 2), x1s[T * 128:(T + 1) * 128, :])
            if ctx_needed:
                self.load_mod(mod, modrows, 1)
                compute_z(plain(ctx[0:128, :], "csrc"), zb[0], "zb0", xts[0], "m_xt0")
                compute_z(plain(ctx[128:256, :], "csrc"), zb[1], "zb1", xts[1], "m_xt1")
                mix_tile(None, (zb[0], "zb0"), (zb[1], "zb1"), 5, None, xts[0], "m_xt0", x1s[n_x * 128:n_x * 128 + 128, :])
                mix_tile((zb[0], "zb0"), (zb[1], "zb1"), None, 6, None, xts[1], "m_xt1", x1s[n_x * 128 + 128:n_x * 128 + 256, :])
            fw.barrier()

    def phase_mla(self, a, modrows, load_q, n_xq, xkeys, ctx, x1s, attn_d, cq_d, ctx_needed):
        nc, fw = self.nc, self.fw
        V = nc.vector
        G = nc.gpsimd
        nqt = n_xq + (2 if ctx_needed else 0)
        NQ = nqt * 128
        QB = 2048
        SCALE = 192.0 ** -0.5

        def plain(src, srckey):
            return lambda xt, xkey: fw.dma("sp", lambda: nc.sync.dma_start(out=xt[:], in_=src), self.dsem(xkey), reads=[srckey], writes=[xkey])

        with ExitStack() as es_outer:
            latT = self.sb(es_outer, "latT", [128, 3, NKT * 128], BF16)
            with ExitStack() as es:
                mod = self.sb(es, "mod", [128, 6 * D], F32)
                w_in = self.sb(es, "w_in", [128, 8, 704], BF16)
                qng = self.sb(es, "qng", [128, 384], F32)
                kvng = self.sb(es, "kvng", [128, 256], F32)
                xts = [self.sb(es, "a_xt%d" % i, [128, D], F32) for i in range(2)]
                rks = [self.sb(es, "a_rk%d" % i, [128, 64], F32) for i in range(2)]
                cqs = [self.sb(es, "a_cqs%d" % i, [128, 3, 128], BF16) for i in range(2)]
                hh = self.sb(es, "a_h", [128, D], F32)
                hT = self.sb(es, "a_hT", [128, 8, 128], BF16)
                junk = self.sb(es, "a_junk", [128, 384], F32)
                ss = self.sb(es, "a_ss", [128, 1], F32)
                lat = self.sb(es, "a_lat", [128, 384], F32)
                rA = self.sb(es, "a_rA", [128, 2, 32], F32)
                rB = self.sb(es, "a_rB", [128, 2, 32], F32)
                fw.dma("pool", lambda: G.dma_start(out=w_in[:], in_=a["w_in"].rearrange("(k p) n -> p k n", p=128)), self.dsem("wload"), sync=True, writes=["w_in"])
                fw.dma("sp", lambda: nc.sync.dma_start(out=qng[:], in_=a["qng"][0:1, :].broadcast_to([128, 384])), self.dsem("misc"), sync=True, writes=["qng"])
                fw.dma("sp", lambda: nc.sync.dma_start(out=kvng[:], in_=a["kvng"][0:1, :].broadcast_to([128, 256])), self.dsem("misc"), sync=True, writes=["kvng"])

                def front(loader, i):
                    xt = xts[i % 2]
                    xk = "a_xt%d" % (i % 2)
                    loader(xt, xk)
                    self.modulate(xt, xk, mod, 0, D, hh, "a_h")
                    self.transpose_bf(hh, "a_h", hT, "a_hT", 8)

                def rms(pt, pn, n, gtile, gkey):
                    fw.op("act", lambda: nc.scalar.activation(out=junk[:, 0:n], in_=pt[:, 0:n], func=AF.Square, accum_out=ss[:, 0:1]),
                          reads=[pn], writes=["a_junk", "a_ss"])
                    fw.op("act", lambda: nc.scalar.activation(out=ss[:], in_=ss[:], func=AF.Sqrt, bias=self.eps_rms[:, 0:1], scale=1.0 / n),
                          reads=["a_ss", "eps"], writes=["a_ss"])
                    fw.op("dve", lambda: V.reciprocal(out=ss[:], in_=ss[:]), reads=["a_ss"], writes=["a_ss"])
                    fw.op("dve", lambda: V.scalar_tensor_tensor(out=lat[:, 0:n], in0=pt[:, 0:n], scalar=ss[:, 0:1], in1=gtile[:, 0:n], op0=ALU.mult, op1=ALU.mult),
                          reads=[pn, "a_ss", gkey], writes=["a_lat"])

                for phase_src in (0, 1):
                    self.load_mod(mod, modrows, phase_src)
                    if phase_src == 0:
                        ktiles = [(plain(xkeys[kt * 128:(kt + 1) * 128, :], "xsrc"), kt) for kt in range(SEQ // 128)]
                        qtiles = [((lambda xt, xk, t=t: load_q(t, xt, xk)), t) for t in range(n_xq)]
                    else:
                        ktiles = [(plain(ctx[j * 128:(j + 1) * 128, :], "csrc"), SEQ // 128 + j) for j in range(2)]
                        qtiles = [(plain(ctx[j * 128:(j + 1) * 128, :], "csrc"), n_xq + j) for j in range(2)] if ctx_needed else []
                    for i, (loader, kt) in enumerate(ktiles):
                        rk = rks[i % 2]
                        rkk = "a_rk%d" % (i % 2)
                        fw.dma("sp", lambda: nc.sync.dma_start(out=rk[:], in_=a["ropek"][kt * 128:(kt + 1) * 128, :]), self.dsem(rkk), writes=[rkk])
                        front(loader, i)
                        pt, pn = self.next_ps()
                        for k in range(8):
                            fw.op("pe", lambda: nc.tensor.matmul(pt[:, 0:320], lhsT=hT[:, k, :], rhs=w_in[:, k, 384:704], start=(k == 0), stop=(k == 7)),
                                  reads=["a_hT", "w_in"], writes=[pn], same_ok=True)
                        rms(pt, pn, 256, kvng, "kvng")
                        zr = pt[:, 256:320].rearrange("p (a b) -> p a b", a=2)
                        fw.op("dve", lambda: V.tensor_tensor(out=rA[:], in0=zr, in1=rk[:, 0:32].unsqueeze(1).broadcast_to([128, 2, 32]), op=ALU.mult),
                              reads=[pn, rkk], writes=["a_rA"])
                        fw.op("dve", lambda: V.tensor_tensor(out=rB[:], in0=zr, in1=rk[:, 32:64].unsqueeze(1).broadcast_to([128, 2, 32]), op=ALU.mult),
                              reads=[pn, rkk], writes=["a_rB"])
                        fw.op("dve", lambda: V.tensor_tensor(out=lat[:, 256:288], in0=rA[:, 0, :], in1=rB[:, 1, :], op=ALU.subtract),
                              reads=["a_rA", "a_rB"], writes=["a_lat"])
                        fw.op("dve", lambda: V.tensor_tensor(out=lat[:, 288:320], in0=rB[:, 0, :], in1=rA[:, 1, :], op=ALU.add),
                              reads=["a_rA", "a_rB"], writes=["a_lat"])
                        pt2, pn2 = self.next_ps()
                        for c in range(2):
                            fw.op("pe", lambda: nc.tensor.transpose(pt2[:, c * 128:(c + 1) * 128], lat[:, c * 128:(c + 1) * 128], self.ident[:]),
                                  reads=["a_lat", "ident"], writes=[pn2], same_ok=True)
                        fw.op("pe", lambda: nc.tensor.transpose(pt2[0:64, 256:384], lat[:, 256:320], self.ident[:]),
                              reads=["a_lat", "ident"], writes=[pn2], same_ok=True)
                        fw.op("act", lambda: nc.scalar.copy(latT[:, 0:2, kt * 128:(kt + 1) * 128], pt2[:, 0:256].rearrange("p (a b) -> p a b", a=2)),
                              reads=[pn2], writes=["latT"])
                        fw.op("act", lambda: nc.scalar.copy(latT[0:64, 2, kt * 128:(kt + 1) * 128], pt2[0:64, 256:384]),
                              reads=[pn2], writes=["latT"])
                    for i, (loader, qt) in enumerate(qtiles):
                        front(loader, i)
                        pt, pn = self.next_ps()
                        for k in range(8):
                            fw.op("pe", lambda: nc.tensor.matmul(pt[:, 0:384], lhsT=hT[:, k, :], rhs=w_in[:, k, 0:384], start=(k == 0), stop=(k == 7)),
                                  reads=["a_hT", "w_in"], writes=[pn], same_ok=True)
                        rms(pt, pn, 384, qng, "qng")
                        pt2, pn2 = self.next_ps()
                        for c in range(3):
                            fw.op("pe", lambda: nc.tensor.transpose(pt2[:, c * 128:(c + 1) * 128], lat[:, c * 128:(c + 1) * 128], self.ident[:]),
                                  reads=["a_lat", "ident"], writes=[pn2], same_ok=True)
                        cs = cqs[i % 2]
                        ck = "a_cqs%d" % (i % 2)
                        fw.op("act", lambda: nc.scalar.copy(cs[:], pt2[:, 0:384].rearrange("p (a b) -> p a b", a=3)), reads=[pn2], writes=[ck])
                        fw.dma("sp", lambda: nc.sync.dma_start(out=cq_d[:, :, qt * 128:(qt + 1) * 128], in_=cs[:]), self.dsem(ck), reads=[ck], writes=["cq_d"])
                fw.barrier()
            with ExitStack() as es:
                w_uq = self.sb(es, "w_uq", [128, 3, 1536], BF16)
                w_rot = self.sb(es, "w_rot", [128, 3, 512], BF16)
                w_ukv = self.sb(es, "w_ukv", [128, 2, 2048], BF16)
                cqT = self.sb(es, "cqT", [128, 3, QB], BF16)
                cosT = self.sb(es, "cosT", [64, QB], F32)
                sinT = self.sb(es, "sinT", [64, QB], F32)
                KnT = self.sb(es, "KnT", [128, NKT * 128], BF16)
                Vs = self.sb(es, "Vs", [128, NKT, 128], BF16)
                qnT = self.sb(es, "qnT", [128, QB], BF16)
                qrT = self.sb(es, "qrT", [64, QB], BF16)
                r1 = self.sb(es, "r1", [64, 512], F32)
                r2 = self.sb(es, "r2", [64, 512], F32)
                PT = [self.sb(es, "PT%d" % i, [128, 512], BF16) for i in range(3)]
                rec = self.sb(es, "rec", [128, 512], F32)
                ao = [self.sb(es, "ao%d" % i, [128, 512], BF16) for i in range(2)]
                Lacc = [self.sb(es, "Lacc%d" % i, [128, 512], F32) for i in range(2)]
                ones_f = self.sb(es, "ones_f", [128, 128], F32)
                fw.op("dve", lambda: V.memset(ones_f[:], 1.0), writes=["ones_f"])
                fw.dma("pool", lambda: G.dma_start(out=w_uq[:], in_=a["w_uq"].rearrange("(k p) n -> p k n", p=128)), self.dsem("wload"), sync=True, writes=["w_uq"])
                fw.dma("pool", lambda: G.dma_start(out=w_ukv[:], in_=a["w_ukv"].rearrange("(k p) n -> p k n", p=128)), self.dsem("wload"), sync=True, writes=["w_ukv"])
                for c in range(3):
                    rv = w_uq[:, c, :].rearrange("p (h f) -> p h f", h=8)
                    ov = w_rot[:, c, :].rearrange("p (h f) -> p h f", h=8)
                    fw.op("dve", lambda: V.tensor_scalar(ov[:, :, 0:32], rv[:, :, 160:192], -1.0, None, ALU.mult), reads=["w_uq"], writes=["w_rot"])
                    fw.op("dve", lambda: V.tensor_copy(ov[:, :, 32:64], rv[:, :, 128:160]), reads=["w_uq"], writes=["w_rot"])
                pbi_evac = [0]

                def evac(dst_ap, src_ap, pn, dkey):
                    pbi_evac[0] += 1
                    if pbi_evac[0] % 2 == 0:
                        fw.op("act", lambda: nc.scalar.copy(dst_ap, src_ap), reads=[pn], writes=[dkey])
                    else:
                        fw.op("dve", lambda: V.tensor_copy(dst_ap, src_ap), reads=[pn], writes=[dkey])

                blocks = []
                for q0 in range(0, n_xq * 128, QB):
                    blocks.append((q0, min(QB, n_xq * 128 - q0), 0, NKT))
                if ctx_needed:
                    blocks.append((n_xq * 128, CTX, SEQ // 128, NKT))
                ci = 0
                for (b0, bn, kt0, kt1) in blocks:
                    fw.dma("sp", lambda: nc.sync.dma_start(out=cqT[:, :, 0:bn], in_=cq_d[:, :, b0:b0 + bn]), self.dsem("cqb"), sync=True, reads=["cq_d"], writes=["cqT"])
                    fw.dma("sp", lambda: nc.sync.dma_start(out=cosT[:, 0:bn], in_=a["ropeq"][:, 0, b0:b0 + bn]), self.dsem("cqb"), sync=True, writes=["ropeq"])
                    fw.dma("sp", lambda: nc.sync.dma_start(out=sinT[:, 0:bn], in_=a["ropeq"][:, 1, b0:b0 + bn]), self.dsem("cqb"), sync=True, writes=["ropeq"])
                    for h in range(8):
                        for kc in range(kt0 * 128, kt1 * 128, 512):
                            n = min(512, kt1 * 128 - kc)
                            pt, pn = self.next_ps(6, 8)
                            for c in range(2):
                                fw.op("pe", lambda: nc.tensor.matmul(pt[:, 0:n], lhsT=w_ukv[:, c, h * 256:h * 256 + 128], rhs=latT[:, c, kc:kc + n], start=(c == 0), stop=(c == 1)),
                                      reads=["w_ukv", "latT"], writes=[pn], same_ok=True)
                            evac(KnT[:, kc:kc + n], pt[:, 0:n], pn, "KnT")
                        for k4 in range(kt0, kt1, 4):
                            n = min(4, kt1 - k4)
                            pt, pn = self.next_ps(6, 8)
                            for j in range(n):
                                kt = k4 + j
                                for c in range(2):
                                    fw.op("pe", lambda: nc.tensor.matmul(pt[:, j * 128:(j + 1) * 128], lhsT=latT[:, c, kt * 128:(kt + 1) * 128],
                                                                         rhs=w_ukv[:, c, h * 256 + 128:h * 256 + 256], start=(c == 0), stop=(c == 1)),
                                          reads=["w_ukv", "latT"], writes=[pn], same_ok=True)
                            evac(Vs[:, k4:k4 + n, :], pt[:, 0:n * 128].rearrange("p (a b) -> p a b", a=n), pn, "Vs")
                        for q0 in range(0, bn, 512):
                            n = min(512, bn - q0)
                            pt, pn = self.next_ps(6, 8)
                            for c in range(3):
                                fw.op("pe", lambda: nc.tensor.matmul(pt[:, 0:n], lhsT=w_uq[:, c, h * 192:h * 192 + 128], rhs=cqT[:, c, q0:q0 + n], start=(c == 0), stop=(c == 2)),
                                      reads=["w_uq", "cqT"], writes=[pn], same_ok=True)
                            evac(qnT[:, q0:q0 + n], pt[:, 0:n], pn, "qnT")
                            pt1, pn1 = self.next_ps(6, 8)
                            for c in range(3):
                                fw.op("pe", lambda: nc.tensor.matmul(pt1[0:64, 0:n], lhsT=w_uq[:, c, h * 192 + 128:h * 192 + 192], rhs=cqT[:, c, q0:q0 + n], start=(c == 0), stop=(c == 2)),
                                      reads=["w_uq", "cqT"], writes=[pn1], same_ok=True)
                            fw.op("dve", lambda: V.tensor_tensor(out=r1[:, 0:n], in0=pt1[0:64, 0:n], in1=cosT[:, q0:q0 + n], op=ALU.mult), reads=[pn1, "ropeq"], writes=["r1"])
                            pt2, pn2 = self.next_ps(6, 8)
                            for c in range(3):
                                fw.op("pe", lambda: nc.tensor.matmul(pt2[0:64, 0:n], lhsT=w_rot[:, c, h * 64:(h + 1) * 64], rhs=cqT[:, c, q0:q0 + n], start=(c == 0), stop=(c == 2)),
                                      reads=["w_rot", "cqT"], writes=[pn2], same_ok=True)
                            fw.op("dve", lambda: V.tensor_tensor(out=r2[:, 0:n], in0=pt2[0:64, 0:n], in1=sinT[:, q0:q0 + n], op=ALU.mult), reads=[pn2, "ropeq"], writes=["r2"])
                            fw.op("dve", lambda: V.tensor_tensor(out=qrT[:, q0:q0 + n], in0=r1[:, 0:n], in1=r2[:, 0:n], op=ALU.add), reads=["r1", "r2"], writes=["qrT"])
                        for q0 in range(0, bn, 512):
                            n = min(512, bn - q0)
                            ci += 1
                            O, On = self.pb[2 + ci % 2], "pb%d" % (2 + ci % 2)
                            L, Ln = self.pb[4 + ci % 2], "pb%d" % (4 + ci % 2)
                            def emit_S(kt):
                                S, Sn = self.next_ps(0, 2)
                                fw.op("pe", lambda: nc.tensor.matmul(S[:, 0:n], lhsT=KnT[:, kt * 128:(kt + 1) * 128], rhs=qnT[:, q0:q0 + n], start=True, stop=False),
                                      reads=["KnT", "qnT"], writes=[Sn], same_ok=True)
                                fw.op("pe", lambda: nc.tensor.matmul(S[:, 0:n], lhsT=latT[0:64, 2, kt * 128:(kt + 1) * 128], rhs=qrT[:, q0:q0 + n], start=False, stop=True),
                                      reads=["latT", "qrT"], writes=[Sn], same_ok=True)
                                return S, Sn

                            nxt = emit_S(kt0)
                            for kt in range(kt0, kt1):
                                S, Sn = nxt
                                if kt + 1 < kt1:
                                    nxt = emit_S(kt + 1)
                                P = PT[kt % 3]
                                Pn = "PT%d" % (kt % 3)
                                fw.op("act", lambda: nc.scalar.activation(out=P[:, 0:n], in_=S[:, 0:n], func=AF.Exp, scale=SCALE), reads=[Sn], writes=[Pn])
                                fw.op("pe", lambda: nc.tensor.matmul(O[:, 0:n], lhsT=Vs[:, kt, :], rhs=P[:, 0:n], start=(kt == kt0), stop=(kt == kt1 - 1)),
                                      reads=["Vs", Pn], writes=[On], same_ok=True)
                                e = (kt - kt0) % 2
                                la = Lacc[e]
                                lk = "Lacc%d" % e
                                if e == 0:
                                    if kt - kt0 < 2:
                                        fw.op("dve", lambda: V.tensor_copy(la[:, 0:n], P[:, 0:n]), reads=[Pn], writes=[lk])
                                    else:
                                        fw.op("dve", lambda: V.tensor_tensor(out=la[:, 0:n], in0=la[:, 0:n], in1=P[:, 0:n], op=ALU.add), reads=[Pn, lk], writes=[lk])
                                else:
                                    if kt - kt0 < 2:
                                        fw.op("pool", lambda: G.tensor_copy(la[:, 0:n], P[:, 0:n]), reads=[Pn], writes=[lk])
                                    else:
                                        fw.op("pool", lambda: G.tensor_tensor(out=la[:, 0:n], in0=la[:, 0:n], in1=P[:, 0:n], op=ALU.add), reads=[Pn, lk], writes=[lk])
                            for e in range(2):
                                fw.op("pe", lambda: nc.tensor.matmul(L[:, 0:n], lhsT=ones_f[:, :], rhs=Lacc[e][:, 0:n], start=(e == 0), stop=(e == 1)),
                                      reads=["ones_f", "Lacc%d" % e], writes=[Ln], same_ok=True)
                            fw.op("dve", lambda: V.reciprocal(out=rec[:, 0:n], in_=L[:, 0:n]), reads=[Ln], writes=["rec"])
                            aot = ao[ci % 2]
                            aok = "ao%d" % (ci % 2)
                            fw.op("dve", lambda: V.tensor_tensor(out=aot[:, 0:n], in0=O[:, 0:n], in1=rec[:, 0:n], op=ALU.mult), reads=[On, "rec"], writes=[aok])
                            fw.dma("sp", lambda: nc.sync.dma_start(out=attn_d[h, :, b0 + q0:b0 + q0 + n], in_=aot[:, 0:n]), self.dsem(aok), reads=[aok], writes=["attn_d"])
                fw.barrier()
        with ExitStack() as es:
            mod = self.sb(es, "mod", [128, 6 * D], F32)
            lng = self.sb(es, "lng", [128, D], F32)
            lnb = self.sb(es, "lnb", [128, D], F32)
            w_o = self.sb(es, "w_o", [128, 8, D], BF16)
            xts = [self.sb(es, "c_xt%d" % i, [128, D], F32) for i in range(2)]
            ats = [self.sb(es, "c_at%d" % i, [128, 8, 128], BF16) for i in range(2)]
            tmp = self.sb(es, "c_tmp", [128, D], F32)
            xo = self.sb(es, "c_xo", [128, D], F32)
            stats = self.sb(es, "c_stats", [128, 2, 6], F32)
            mv = self.sb(es, "c_mv", [128, 2], F32)
            sd = self.sb(es, "c_sd", [128, 1], F32)
            fw.dma("pool", lambda: G.dma_start(out=w_o[:], in_=a["w_o"].rearrange("(k p) n -> p k n", p=128)), self.dsem("wload"), sync=True, writes=["w_o"])
            fw.dma("sp", lambda: nc.sync.dma_start(out=lng[:], in_=a["lng"][0:1, :].broadcast_to([128, D])), self.dsem("misc"), sync=True, writes=["lnp"])
            fw.dma("sp", lambda: nc.sync.dma_start(out=lnb[:], in_=a["lnb"][0:1, :].broadcast_to([128, D])), self.dsem("misc"), sync=True, writes=["lnp"])
            for t in range(nqt):
                if t == 0:
                    self.load_mod(mod, modrows, 0)
                if t == n_xq:
                    self.load_mod(mod, modrows, 1)
                xt = xts[t % 2]
                xk = "c_xt%d" % (t % 2)
                at = ats[t % 2]
                ak = "c_at%d" % (t % 2)
                if t < n_xq:
                    load_q(t, xt, xk)
                else:
                    plain(ctx[(t - n_xq) * 128:(t - n_xq + 1) * 128, :], "csrc")(xt, xk)
                fw.dma("sp", lambda: nc.sync.dma_start(out=at[:], in_=attn_d[:, :, t * 128:(t + 1) * 128].rearrange("h p t -> p h t")),
                       self.dsem(ak), reads=["attn_d"], writes=[ak])
                yp = []
                for n in range(2):
                    pt, pn = self.next_ps()
                    for h in range(8):
                        fw.op("pe", lambda: nc.tensor.matmul(pt[:, :], lhsT=at[:, h, :], rhs=w_o[:, h, n * 512:(n + 1) * 512], start=(h == 0), stop=(h == 7)),
                              reads=[ak, "w_o"], writes=[pn], same_ok=True)
                    yp.append((pt, pn))
                self.post_norm(xt, xk, [yp[0][0][:, :], yp[1][0][:, :]], [yp[0][1], yp[1][1]], mod, 2 * D, lng[:], lnb[:],
                               tmp, "c_tmp", xo, "c_xo", (stats, mv, sd))
                fw.dma("sp", lambda: nc.sync.dma_start(out=x1s[t * 128:(t + 1) * 128, :], in_=xo[:]), self.dsem("xo"), reads=["c_xo"], writes=["x1s"])
            fw.barrier()


NH = NT + 1


def build_program():
    nc = bass.Bass("TRN2", target_bir_lowering=False)
    A = []
    for l in range(DEPTH):
        a = {}

        def inp(name, shape, dt=F32, a=a, l=l):
            a[name] = nc.dram_tensor("%s_%d" % (name, l), list(shape), dt, kind="ExternalInput").ap()

        inp("wmod", [D, 6 * D])
        inp("bmod", [1, 6 * D])
        inp("lng", [2, D])
        inp("lnb", [2, D])
        inp("wq", [D, 2048])
        inp("kk", [16, 128, 128])
        inp("pu", [16384, D])
        inp("pv", [16384, D])
        if l % 2 == 0:
            nq = (SEQ + CTX) if l == 0 else NH * 128
            inp("w_in", [D, 704])
            inp("qng", [1, 384])
            inp("kvng", [1, 256])
            inp("w_uq", [384, 1536])
            inp("w_ukv", [256, 2048])
            inp("w_o", [D, D])
            inp("ropeq", [64, 2, nq])
        else:
            inp("pw_in", [D, D])
            inp("pw_grp", [4, 256, 256])
            inp("pscale", [128, 8])
            inp("pw_out", [D, D])
            inp("band", [128, 4, 9, 128])
        A.append(a)
    xfull = nc.dram_tensor("xfull", [SEQ, D], F32, kind="ExternalInput").ap()
    ctx0 = nc.dram_tensor("ctx", [CTX, D], F32, kind="ExternalInput").ap()
    ccol = nc.dram_tensor("ccol", [128, 16], F32, kind="ExternalInput").ap()
    ropek = nc.dram_tensor("ropek", [SEQ + CTX, 64], F32, kind="ExternalInput").ap()
    idx2_d = nc.dram_tensor("idx2", [128, NH], F32, kind="ExternalInput").ap()
    xout = nc.dram_tensor("xout", [TOK, D], F32, kind="ExternalOutput").ap()
    for a in A:
        a["ccol"] = ccol
        a["ropek"] = ropek

    def scratch(name, shape, dt=F32):
        return nc.dram_tensor(name, list(shape), dt, kind="Internal").ap()

    modrows = [scratch("modrows%d" % l, [2, 6 * D]) for l in range(DEPTH)]
    X1 = scratch("X1", [SEQ, D])
    C1 = scratch("C1", [CTX, D])
    X2 = scratch("X2", [SEQ, D])
    C2 = scratch("C2", [CTX, D])
    X3 = scratch("X3", [NH * 128, D])
    x1s = [scratch("x1s0", [SEQ + CTX, D]), scratch("x1s1", [SEQ + CTX, D]), scratch("x1s2", [NH * 128, D]), scratch("x1s3", [TOK, D])]
    attn0 = scratch("attn0", [8, 128, SEQ + CTX], BF16)
    cq0 = scratch("cq0", [128, 3, SEQ + CTX], BF16)
    attn2 = scratch("attn2", [8, 128, NH * 128], BF16)
    cq2 = scratch("cq2", [128, 3, NH * 128], BF16)
    NS = SEQ // 128
    uv_bf = scratch("uv_bf", [16384, 2 * D], BF16)
    with ExitStack() as es:
        p = Prog(nc, es)
        fw = p.fw
        idx2 = p.sb(es, "idx2", [128, NH], U32)
        idx2f = p.sb(es, "idx2f", [128, NH], F32)
        fw.dma("sp", lambda: nc.sync.dma_start(out=idx2f[:], in_=idx2_d[:, :]), p.dsem("misc"), sync=True, writes=["idx2f"])
        fw.op("dve", lambda: nc.vector.tensor_copy(idx2[:], idx2f[:]), reads=["idx2f"], writes=["idx2"])

        def plain_loader(src):
            def f(t, xt, xkey):
                fw.dma("sp", lambda: nc.sync.dma_start(out=xt[:], in_=src[t * 128:(t + 1) * 128, :]), p.dsem(xkey), reads=["xsrc"], writes=[xkey])
            return f

        def gather_loader(src):
            def f(t, xt, xkey):
                fw.dma("pool", lambda: nc.gpsimd.indirect_dma_start(out=xt[:], out_offset=None, in_=src[:, :],
                                                                    in_offset=bass.IndirectOffsetOnAxis(ap=idx2[:, t:t + 1], axis=0)),
                       p.dsem(xkey), reads=["xsrc", "idx2"], writes=[xkey])
            return f

        def dst_rows(xt_, ct_, n_x):
            def f(t):
                if t < n_x:
                    return xt_[t * 128:(t + 1) * 128, :], "xdst"
                return ct_[(t - n_x) * 128:(t - n_x + 1) * 128, :], "cdst"
            return f

        p.phase_mod(A[0], modrows[0])
        p.phase_mla(A[0], modrows[0], plain_loader(xfull), NS, xfull, ctx0, x1s[0], attn0, cq0, True)
        p.phase_cvt(A[0], uv_bf)
        p.phase_peer(A[0], modrows[0], x1s[0], NS, dst_rows(X1, C1, NS), True, uv_bf)
        p.phase_mod(A[1], modrows[1])
        p.phase_pool(A[1], modrows[1], plain_loader(X1), NS, None, C1, x1s[1], True)
        p.phase_cvt(A[1], uv_bf)
        p.phase_peer(A[1], modrows[1], x1s[1], NS, dst_rows(X2, C2, NS), True, uv_bf)
        p.phase_mod(A[2], modrows[2])
        p.phase_mla(A[2], modrows[2], gather_loader(X2), NH, X2, C2, x1s[2], attn2, cq2, False)
        p.phase_cvt(A[2], uv_bf)
        p.phase_peer(A[2], modrows[2], x1s[2], NH, dst_rows(X3, None, NH), False, uv_bf)
        p.phase_mod(A[3], modrows[3])
        p.phase_pool(A[3], modrows[3], plain_loader(X3), NT, X3[TOK:TOK + 16, :], None, x1s[3], False)
        p.phase_cvt(A[3], uv_bf)
        p.phase_peer(A[3], modrows[3], x1s[3], NT, dst_rows(xout, None, NT), False, uv_bf)
        fw.barrier(["sp"])
    return nc


def _rope_tables():
    rows = SEQ // 64
    row = np.repeat(np.arange(rows), 64).astype(np.float32)
    col = np.tile(np.arange(64), rows).astype(np.float32)
    freqs = (np.float32(10000.0) ** (-np.arange(16, dtype=np.float32) / np.float32(16))).astype(np.float32)
    ang = np.concatenate([row[:, None] * freqs, col[:, None] * freqs], axis=-1).astype(np.float32)
    return np.cos(ang).astype(np.float32), np.sin(ang).astype(np.float32)


def _band_mats(q):
    wins = (2, 4, 8, 16)
    band = np.zeros((128, 4, 9, 128), np.float32)

    def fill(g, L, t_glob0, kind):
        w = wins[g]
        m = np.zeros((128, 128), np.float32)
        for t in range(128):
            tg = t_glob0 + t
            lo = max(tg - w // 2, 0)
            hi = min(tg + w // 2, L)
            cnt = hi - lo
            for tp in range(lo, hi):
                r = tp - (t_glob0 + kind * 128)
                if 0 <= r < 128:
                    m[r, t] += 1.0 / cnt
            if kind == 0:
                m[t, t] -= 1.0
        return m

    for g in range(4):
        mid0 = 128 * 10
        band[:, g, 0, :] = fill(g, SEQ, mid0, 0)
        band[:, g, 1, :] = fill(g, SEQ, mid0, -1)
        band[:, g, 2, :] = fill(g, SEQ, mid0, +1)
        band[:, g, 3, :] = fill(g, SEQ, 0, 0) if q == 0 else band[:, g, 0, :]
        band[:, g, 4, :] = fill(g, SEQ, SEQ - 128, 0) if q == 3 else band[:, g, 0, :]
        band[:, g, 5, :] = fill(g, CTX, 0, 0)
        band[:, g, 6, :] = fill(g, CTX, CTX - 128, 0)
        if q != 0:
            band[0:8, g, 7, :] = band[120:128, g, 1, :]
        if q != 3:
            band[8:16, g, 8, :] = band[0:8, g, 2, :]
    return band


def _col_layout(v):
    return np.ascontiguousarray(v.reshape(8, 128).T)


def _band_full():
    b = _band_mats(0)
    b[:, :, 4] = _band_mats(3)[:, :, 4]
    b[:, :, 7:9] = 0.0
    return b


def kernel(x, c, ctx, c_ctx, w_mod, b_mod, ln_g, ln_b, mla_w_in, mla_q_norm, mla_kv_norm, mla_w_uq, mla_w_ukv, mla_w_o,
           pool_w_in, pool_w_grp, pool_scale, pool_w_out, peer_w_q, peer_k1, peer_k2, peer_u, peer_v):
    f = lambda t: np.ascontiguousarray(np.asarray(t, dtype=np.float32))
    x = f(x)
    ctx = f(ctx)
    cos, sin = _rope_tables()
    ropek = np.zeros((SEQ + CTX, 64), np.float32)
    ropek[:SEQ, :32] = cos
    ropek[:SEQ, 32:] = sin
    ropek[SEQ:, :32] = 1.0

    def rope_q(pos, n_extra):
        n = len(pos)
        rq = np.zeros((64, 2, n + n_extra), np.float32)
        rq[0:32, 0, :n] = cos[pos].T
        rq[32:64, 0, :n] = cos[pos].T
        rq[0:32, 1, :n] = sin[pos].T
        rq[32:64, 1, :n] = sin[pos].T
        rq[:, 0, n:] = 1.0
        return rq

    shared = {}
    for l in range(DEPTH):
        j = l // 2
        d = {
            "wmod": f(w_mod[l]), "bmod": f(b_mod[l])[None, :], "lng": f(ln_g[l]), "lnb": f(ln_b[l]), "wq": f(peer_w_q[l]),
            "kk": np.ascontiguousarray(np.stack([f(peer_k1[l]), f(peer_k2[l])], axis=1).reshape(16, 128, 128)),
            "pu": f(peer_u[l]), "pv": f(peer_v[l]),
        }
        if l % 2 == 0:
            d.update({"w_in": f(mla_w_in[j]), "qng": f(mla_q_norm[j])[None, :], "kvng": f(mla_kv_norm[j])[None, :],
                      "w_uq": f(mla_w_uq[j]), "w_ukv": f(mla_w_ukv[j]), "w_o": f(mla_w_o[j])})
        else:
            d.update({"pw_in": f(pool_w_in[j]), "pw_grp": f(pool_w_grp[j]), "pscale": _col_layout(f(pool_scale[j])), "pw_out": f(pool_w_out[j])})
        for k, v in d.items():
            shared["%s_%d" % (k, l)] = v
    shared["ropek"] = ropek
    shared["ropeq_0"] = rope_q(np.arange(SEQ), CTX)
    shared["band_1"] = _band_full()
    in_maps = []
    for core in range(NCORE):
        b, q = core // 4, core % 4
        idx = np.zeros((128, NH), np.int64)
        for t in range(NT):
            idx[:, t] = q * TOK + t * 128 + np.arange(128)
        idx[:, NT] = q * TOK
        idx[0:8, NT] = np.clip(q * TOK - 8 + np.arange(8), 0, SEQ - 1)
        idx[8:16, NT] = np.clip((q + 1) * TOK + np.arange(8), 0, SEQ - 1)
        m = dict(shared)
        m.update({
            "xfull": x[b], "ctx": ctx[b],
            "ccol": np.concatenate([_col_layout(f(c)[b]), _col_layout(f(c_ctx))], axis=1),
            "idx2": idx.astype(np.float32),
            "ropeq_2": rope_q(idx.T.reshape(-1), 0),
            "band_3": _band_mats(q),
        })
        in_maps.append(m)
    nc = build_program()
    res = run_bass_kernel_spmd(nc, in_maps, core_ids=list(range(NCORE)))
    out = np.empty_like(x)
    for core in range(NCORE):
        b, q = core // 4, core % 4
        out[b, q * TOK:(q + 1) * TOK] = res.results[core]["xout"]
    return out
```

```python
import math
from contextlib import ExitStack

import numpy as np
import concourse.bass as bass
import concourse.mybir as mybir
from concourse.bass_utils import run_bass_kernel_spmd

F32 = mybir.dt.float32
BF16 = mybir.dt.bfloat16
U32 = mybir.dt.uint32
AF = mybir.ActivationFunctionType
ALU = mybir.AluOpType
AX = mybir.AxisListType

D = 1024
DEPTH = 4
SEQ = 8192
CTX = 256
NCORE = 8
TOK = 2048
NT = TOK // 128
NKT = (SEQ + CTX) // 128
ALPHA = (2.0 * DEPTH) ** 0.25
LN_EPS = 1e-5
RMS_EPS = 1e-6
NU = 8


class Buf:
    __slots__ = ("last_w", "readers")

    def __init__(self):
        self.last_w = None
        self.readers = {}


class FW:
    ENG = ("pe", "dve", "act", "pool", "sp")
    PAGE = 24000

    def __init__(self, nc, es, n_dma_sems=64):
        self.nc = nc
        self.es = es
        self.eng = {"pe": nc.tensor, "dve": nc.vector, "act": nc.scalar, "pool": nc.gpsimd, "sp": nc.sync}
        self.nsem = 0
        self.sem = {k: self._newsem() for k in self.ENG}
        self.cnt = {k: 0 for k in self.ENG}
        self.last = {k: None for k in self.ENG}
        self.known = {k: {} for k in self.ENG}
        self.dsems = []
        self.dcnt = []
        self.dall = []
        self.bufs = {}
        self.ninst = 0

    def _newsem(self):
        self.nsem += 1
        return self.es.enter_context(self.nc.semaphore("fs%d" % self.nsem))

    def buf(self, name):
        b = self.bufs.get(name)
        if b is None:
            b = Buf()
            self.bufs[name] = b
        return b

    def dsem(self):
        self.dsems.append(self._newsem())
        self.dcnt.append(0)
        return len(self.dsems) - 1

    def _deps(self, ek, reads, writes, same_ok):
        deps = {}
        for b in reads:
            t = self.buf(b).last_w
            if t is not None and deps.get(t[0], 0) < t[1]:
                deps[t[0]] = t[1]
        for b in writes:
            bb = self.buf(b)
            t = bb.last_w
            if t is not None and deps.get(t[0], 0) < t[1]:
                deps[t[0]] = t[1]
            for s, v in bb.readers.items():
                if deps.get(s, 0) < v:
                    deps[s] = v
        eng = self.eng[ek]
        kn = self.known[ek]
        own = self.sem[ek]
        for s, v in deps.items():
            if same_ok and s is own:
                continue
            if kn.get(s, 0) < v:
                eng.wait_ge(s, v)
                kn[s] = v
                self.ninst += 1

    def _record(self, tok, reads, writes):
        s, v = tok
        for b in reads:
            bb = self.buf(b)
            if bb.readers.get(s, 0) < v:
                bb.readers[s] = v
        for b in writes:
            bb = self.buf(b)
            bb.last_w = tok
            bb.readers = {}

    def op(self, ek, fn, reads=(), writes=(), same_ok=False):
        if self.cnt[ek] >= self.PAGE:
            self.sem[ek] = self._newsem()
            self.cnt[ek] = 0
        self._deps(ek, reads, writes, same_ok)
        inst = fn()
        self.cnt[ek] += 1
        inst.then_inc(self.sem[ek], 1)
        self.ninst += 1
        tok = (self.sem[ek], self.cnt[ek])
        self.last[ek] = tok
        self._record(tok, reads, writes)

    def dma(self, ek, fn, di, reads=(), writes=(), sync=False):
        if self.dcnt[di] >= self.PAGE:
            self.dall.append((self.dsems[di], self.dcnt[di]))
            self.dsems[di] = self._newsem()
            self.dcnt[di] = 0
        self._deps(ek, reads, writes, False)
        inst = fn()
        self.dcnt[di] += 16
        inst.then_inc(self.dsems[di], 16)
        self.ninst += 1
        self._record((self.dsems[di], self.dcnt[di]), reads, writes)
        if sync:
            self.eng[ek].wait_ge(self.dsems[di], self.dcnt[di])
            self.known[ek][self.dsems[di]] = self.dcnt[di]

    def barrier(self, engs=None):
        for ek in (engs or self.ENG):
            eng = self.eng[ek]
            kn = self.known[ek]
            toks = [self.last[k2] for k2 in self.ENG if k2 != ek and self.last[k2] is not None]
            toks += self.dall
            toks += [(s, v) for s, v in zip(self.dsems, self.dcnt) if v]
            for s, v in toks:
                if kn.get(s, 0) < v:
                    eng.wait_ge(s, v)
                    kn[s] = v


class Prog:
    def __init__(self, nc, es):
        self.nc = nc
        self.es = es
        self.fw = FW(nc, es)
        fw = self.fw
        nc_ = nc
        self.pb = [es.enter_context(nc.psum_tensor("pb%d" % i, [128, 512], F32)) for i in range(8)]
        self.pbi = 0
        self.ident = self.sb(es, "ident", [128, 128], F32)
        self.iota16 = self.sb(es, "iota16", [128, 16], F32)
        self.ones_bf = self.sb(es, "ones_bf", [128, 128], BF16)
        self.identb = self.sb(es, "identb", [128, 128], BF16)
        self.eps_ln = self.sb(es, "eps_ln", [128, 1], F32)
        self.eps_rms = self.sb(es, "eps_rms", [128, 1], F32)
        iot = self.sb(es, "iot", [128, 128], F32)
        pidx = self.sb(es, "pidx", [128, 1], F32)
        G = nc.gpsimd
        fw.op("pool", lambda: G.iota(iot[:], [[1, 128]], base=0, channel_multiplier=0,
                                     allow_small_or_imprecise_dtypes=True), writes=["iot"])
        fw.op("pool", lambda: G.iota(pidx[:], [[1, 1]], base=0, channel_multiplier=1,
                                     allow_small_or_imprecise_dtypes=True), writes=["pidx"])
        fw.op("pool", lambda: G.iota(self.iota16[:], [[1, 16]], base=0, channel_multiplier=0,
                                     allow_small_or_imprecise_dtypes=True), writes=["iota16"])
        fw.op("dve", lambda: nc.vector.tensor_scalar(self.ident[:], iot[:], pidx[:, 0:1], None, ALU.is_equal),
              reads=["iot", "pidx"], writes=["ident"])
        fw.op("dve", lambda: nc.vector.memset(self.ones_bf[:], 1.0), writes=["ones_bf"])
        fw.op("dve", lambda: nc.vector.tensor_copy(self.identb[:], self.ident[:]), reads=["ident"], writes=["identb"])
        fw.op("dve", lambda: nc.vector.memset(self.eps_ln[:], LN_EPS), writes=["eps"])
        fw.op("dve", lambda: nc.vector.memset(self.eps_rms[:], RMS_EPS), writes=["eps"])
        self.ds = {}

    def sb(self, es, name, shape, dtype):
        self.nsb = getattr(self, "nsb", 0) + 1
        return es.enter_context(self.nc.sbuf_tensor("sb%d_%s" % (self.nsb, name), list(shape), dtype))

    def dsem(self, name):
        if name not in self.ds:
            self.ds[name] = self.fw.dsem()
        return self.ds[name]

    def next_ps(self, lo=0, hi=None):
        if hi is None:
            hi = getattr(self, "ps_hi", 8)
        i = lo + self.pbi % (hi - lo)
        self.pbi += 1
        return self.pb[i], "pb%d" % i

    def transpose_bf(self, src, srckey, dst, dstkey, nchunk, rows=128):
        nc, fw = self.nc, self.fw
        for c0 in range(0, nchunk, 4):
            n = min(4, nchunk - c0)
            pt, pn = self.next_ps()
            for j in range(n):
                c = c0 + j
                fw.op("pe", lambda: nc.tensor.transpose(pt[:, j * rows:(j + 1) * rows], src[0:rows, c * 128:(c + 1) * 128],
                                                        self.ident[0:rows, 0:rows]),
                      reads=[srckey, "ident"], writes=[pn], same_ok=True)
            fw.op("act", lambda: nc.scalar.copy(dst[:, c0:c0 + n, :], pt[:, 0:n * rows].rearrange("p (a b) -> p a b", a=n)),
                  reads=[pn], writes=[dstkey])

    def modulate(self, xt, xkey, mod, shift_off, scale_off, out, outkey, rows=128):
        nc, fw = self.nc, self.fw
        V = nc.vector
        fw.op("dve", lambda: V.tensor_tensor(out=out[0:rows, :], in0=xt[0:rows, :], in1=mod[0:rows, scale_off:scale_off + D], op=ALU.mult),
              reads=[xkey, "mod"], writes=[outkey])
        fw.op("pool", lambda: nc.gpsimd.tensor_tensor(out=out[0:rows, :], in0=out[0:rows, :], in1=mod[0:rows, shift_off:shift_off + D], op=ALU.add),
              reads=[outkey, "mod"], writes=[outkey])

    def post_norm(self, xt, xkey, y_parts, ykeys, mod, gate_off, lng, lnb, tmp, tmpkey, out, outkey, sm):
        nc, fw = self.nc, self.fw
        V = nc.vector
        for i, yp in enumerate(y_parts):
            fw.op("dve", lambda: V.tensor_tensor(out=tmp[:, i * 512:(i + 1) * 512], in0=yp, in1=mod[:, gate_off + i * 512:gate_off + (i + 1) * 512], op=ALU.mult),
                  reads=[ykeys[i], "mod"], writes=[tmpkey])
        fw.op("dve", lambda: V.scalar_tensor_tensor(out=tmp[:], in0=xt[:], scalar=ALPHA, in1=tmp[:], op0=ALU.mult, op1=ALU.add),
              reads=[xkey, tmpkey], writes=[tmpkey])
        stats, mv, sd = sm
        for i in range(2):
            fw.op("dve", lambda: V.bn_stats(out=stats[:, i, :], in_=tmp[:, i * 512:(i + 1) * 512]), reads=[tmpkey], writes=["ln_stats%d" % i])
        fw.op("dve", lambda: V.bn_aggr(out=mv[:], in_=stats[:].rearrange("p a b -> p (a b)")), reads=["ln_stats0", "ln_stats1"], writes=["ln_mv"])
        fw.op("act", lambda: nc.scalar.activation(out=sd[:], in_=mv[:, 1:2], func=AF.Sqrt, bias=self.eps_ln[:, 0:1]), reads=["ln_mv", "eps"], writes=["ln_sd"])
        fw.op("dve", lambda: V.reciprocal(out=sd[:], in_=sd[:]), reads=["ln_sd"], writes=["ln_sd"])
        fw.op("dve", lambda: V.tensor_scalar(tmp[:], tmp[:], mv[:, 0:1], sd[:, 0:1], ALU.subtract, ALU.mult),
              reads=[tmpkey, "ln_mv", "ln_sd"], writes=[tmpkey])
        fw.op("pool", lambda: nc.gpsimd.tensor_tensor(out=tmp[:], in0=tmp[:], in1=lng, op=ALU.mult), reads=[tmpkey, "lnp"], writes=[tmpkey])
        fw.op("dve", lambda: V.tensor_tensor(out=out[:], in0=tmp[:], in1=lnb, op=ALU.add), reads=[tmpkey, "lnp"], writes=[outkey])

    def load_mod(self, mod, modrows, which):
        nc, fw = self.nc, self.fw
        fw.dma("sp", lambda: nc.sync.dma_start(out=mod[:], in_=modrows[which:which + 1, :].broadcast_to([128, 6 * D])),
               self.dsem("mod"), reads=["modrows"], writes=["mod"])

    def phase_mod(self, a, modrows):
        nc, fw = self.nc, self.fw
        V = nc.vector
        with ExitStack() as es:
            ccol = self.sb(es, "ccol", [128, 16], F32)
            scol = self.sb(es, "scol", [128, 16], F32)
            brow = self.sb(es, "brow", [1, 6 * D], F32)
            mrow = self.sb(es, "mrow", [1, 2, 6 * D], F32)
            wch = [self.sb(es, "wch%d" % i, [128, 8, 512], F32) for i in range(2)]
            fw.dma("sp", lambda: nc.sync.dma_start(out=ccol[:], in_=a["ccol"][:, :]), self.dsem("misc"), sync=True, writes=["ccol"])
            fw.dma("sp", lambda: nc.sync.dma_start(out=brow[:], in_=a["bmod"][:, :]), self.dsem("misc"), sync=True, writes=["brow"])
            fw.op("act", lambda: nc.scalar.activation(out=scol[:], in_=ccol[:], func=AF.Silu), reads=["ccol"], writes=["scol"])
            wv = a["wmod"].rearrange("(k p) n -> p k n", p=128)
            for n in range(12):
                w = wch[n % 2]
                wk = "wch%d" % (n % 2)
                fw.dma("sp", lambda: nc.sync.dma_start(out=w[:], in_=wv[:, :, n * 512:(n + 1) * 512]), self.dsem(wk), writes=[wk])
                plus1 = 1.0 if n in (2, 3, 8, 9) else 0.0
                for which in range(2):
                    pt, pn = self.next_ps()
                    for k in range(8):
                        fw.op("pe", lambda: nc.tensor.matmul(pt[0:1, :], lhsT=scol[:, which * 8 + k:which * 8 + k + 1], rhs=w[:, k, :],
                                                             start=(k == 0), stop=(k == 7)),
                              reads=["scol", wk], writes=[pn], same_ok=True)
                    fw.op("dve", lambda: V.scalar_tensor_tensor(out=mrow[0:1, which, n * 512:(n + 1) * 512], in0=pt[0:1, :], scalar=plus1,
                                                                in1=brow[0:1, n * 512:(n + 1) * 512], op0=ALU.add, op1=ALU.add),
                          reads=[pn, "brow"], writes=["mrow"])
            fw.dma("sp", lambda: nc.sync.dma_start(out=modrows[:, :], in_=mrow[0:1, :, :].rearrange("p a n -> p (a n)")),
                   self.dsem("misc"), sync=True, reads=["mrow"], writes=["modrows"])
            fw.barrier()

    def phase_cvt(self, a, uv_bf):
        nc, fw = self.nc, self.fw
        with ExitStack() as es:
            st = [self.sb(es, "cvt%d" % i, [128, 4, D], BF16) for i in range(4)]
            i = 0
            for (src, dst) in ((a["pu"], uv_bf[:, 0:D]), (a["pv"], uv_bf[:, D:2 * D])):
                for c in range(16384 // 512):
                    t = st[i % 4]
                    k = "cvt%d" % (i % 4)
                    fw.dma("pool", lambda: nc.gpsimd.dma_start(out=t[:], in_=src[c * 512:(c + 1) * 512, :].rearrange("(k p) d -> p k d", p=128)),
                           self.dsem(k + "l"), writes=[k])
                    fw.dma("sp", lambda: nc.sync.dma_start(out=dst[c * 512:(c + 1) * 512, :].rearrange("(k p) d -> p k d", p=128), in_=t[:]),
                           self.dsem(k + "s"), reads=[k], writes=["tabbf"])
                    i += 1
            fw.barrier()

    def phase_peer(self, a, modrows, x1s, n_x, dst_fn, ctx_needed, uv_bf):
        nc, fw = self.nc, self.fw
        V = nc.vector
        with ExitStack() as es:
            mod = self.sb(es, "mod", [128, 3 * D], F32)
            lng = self.sb(es, "lng", [128, D], F32)
            lnb = self.sb(es, "lnb", [128, D], F32)
            wq = self.sb(es, "wq_bf", [128, 8, 2048], BF16)
            kT = self.sb(es, "kT", [128, 16, 128], BF16)
            xts = [self.sb(es, "p_xt%d" % i, [128, D], F32) for i in range(2)]
            h2 = self.sb(es, "p_h2", [128, D], F32)
            tmp = h2
            h2bs = [self.sb(es, "p_h2b%d" % i, [128, D], BF16) for i in range(2)]
            xo = self.sb(es, "p_xo", [128, D], F32)
            h2T = self.sb(es, "p_h2T", [128, 8, 128], BF16)
            qT = self.sb(es, "p_qT", [128, 16, 128], BF16)
            arena = self.sb(es, "p_arena", [128, 4, 2048], F32)
            s_sb = arena[:, 0, :].rearrange("p (g n) -> p g n", g=16)
            s_wk = arena[:, 1, :].rearrange("p (g n) -> p g n", g=16)
            cand = arena[:, 2, :].rearrange("p (h n) -> p h n", h=8)
            cand_wk = arena[:, 3, :].rearrange("p (h n) -> p h n", h=8)
            oh = arena[:, 0, :].rearrange("p (h k a) -> p h k a", h=8, k=16)
            oh2 = arena[:, 1, :].rearrange("p (h k a) -> p h k a", h=8, k=16)
            m1 = self.sb(es, "p_m1", [128, 16, 16], F32)
            i1 = self.sb(es, "p_i1", [128, 16, 16], U32)
            i1f = self.sb(es, "p_i1f", [128, 16, 16], F32)
            m2 = self.sb(es, "p_m2", [128, 8, 16], F32)
            p2 = self.sb(es, "p_p2", [128, 8, 16], U32)
            pa = self.sb(es, "p_pa", [128, 8, 16], U32)
            pbb = self.sb(es, "p_pb", [128, 8, 16], U32)
            paf = self.sb(es, "p_paf", [128, 8, 16], F32)
            pbf = self.sb(es, "p_pbf", [128, 8, 16], F32)
            sel1 = self.sb(es, "p_sel1", [128, 8, 16], F32)
            sel2 = self.sb(es, "p_sel2", [128, 8, 16], F32)
            idxf = self.sb(es, "p_idxf", [128, 128], F32)
            idxus = [self.sb(es, "p_idxu%d" % i, [128, 128], U32) for i in range(2)]
            ee = self.sb(es, "p_ee", [128, 8, 16], F32)
            zz = self.sb(es, "p_zz", [128, 8], F32)
            ggs = [self.sb(es, "p_gg%d" % i, [128, 8, 16], F32) for i in range(2)]
            apre = self.sb(es, "p_apre", [128, 128], F32)
            wgt = self.sb(es, "p_wgt", [128, 128], F32)
            prod = [self.sb(es, "p_prod%d" % i, [128, D], BF16) for i in range(4)]
            with ExitStack() as es_k:
                kraw = self.sb(es_k, "kraw", [128, 16, 128], F32)
                fw.dma("sp", lambda: nc.sync.dma_start(out=kraw[:], in_=a["kk"].rearrange("g n c -> n g c")), self.dsem("misc"), sync=True, writes=["kraw"])
                for g4 in range(4):
                    pt, pn = self.next_ps()
                    for gg in range(4):
                        g = g4 * 4 + gg
                        fw.op("pe", lambda: nc.tensor.transpose(pt[:, gg * 128:(gg + 1) * 128], kraw[:, g, :], self.ident[:]),
                              reads=["kraw", "ident"], writes=[pn], same_ok=True)
                    fw.op("act", lambda: nc.scalar.copy(kT[:, g4 * 4:(g4 + 1) * 4, :], pt[:].rearrange("p (a b) -> p a b", a=4)),
                          reads=[pn], writes=["kT"])

                fw.barrier()
            NSLOT = 16
            GJ = 2
            cb = [self.sb(es, "p_cb%d" % i, [128, 2 * D], BF16) for i in range(NSLOT)]
            dgs = [self.sb(es, "p_dg%d" % i, [128, GJ, 128], BF16) for i in range(2)]
            ge = self.sb(es, "p_ge", [128, 128], F32)
            cbs = [self.dsem("cb%d" % i) for i in range(NSLOT)]
            self.ps_hi = 6
            accp = [(self.pb[6], "pb6"), (self.pb[7], "pb7")]
            stats = self.sb(es, "p_stats", [128, 2, 6], F32)
            mv = self.sb(es, "p_mv", [128, 2], F32)
            sd = self.sb(es, "p_sd", [128, 1], F32)

            fw.dma("pool", lambda: nc.gpsimd.dma_start(out=wq[:], in_=a["wq"].rearrange("(k p) n -> p k n", p=128)),
                   self.dsem("wload"), sync=True, writes=["wq"])
            fw.dma("sp", lambda: nc.sync.dma_start(out=lng[:], in_=a["lng"][1:2, :].broadcast_to([128, D])), self.dsem("misc"), sync=True, writes=["lnp"])
            fw.dma("sp", lambda: nc.sync.dma_start(out=lnb[:], in_=a["lnb"][1:2, :].broadcast_to([128, D])), self.dsem("misc"), sync=True, writes=["lnp"])
            def load_mod3(which):
                fw.dma("sp", lambda: nc.sync.dma_start(out=mod[:], in_=modrows[which:which + 1, 3 * D:6 * D].broadcast_to([128, 3 * D])),
                       self.dsem("mod"), reads=["modrows"], writes=["mod"])

            SSB = ["s_sb%d" % g for g in range(4)]
            SWK = ["s_wk%d" % g for g in range(16)]
            n_tiles = n_x + (2 if ctx_needed else 0)
            ggfs = [g_[:].rearrange("p h k -> p (h k)") for g_ in ggs]

            def front(t):
                b = t % 2
                xt, h2b, idxu, gg_ = xts[b], h2bs[b], idxus[b], ggs[b]
                xk, hk, ik, gk = "p_xt%d" % b, "h2b_%d" % b, "idxu%d" % b, "gg%d" % b
                src = x1s[t * 128:(t + 1) * 128, :]
                fw.dma("sp", lambda: nc.sync.dma_start(out=xt[:], in_=src), self.dsem(xk), reads=["x1s"], writes=[xk])
                yield
                self.modulate(xt, xk, mod, 0, D, h2, "h2")
                fw.op("act", lambda: nc.scalar.copy(h2b[:], h2[:]), reads=["h2"], writes=[hk])
                yield
                self.transpose_bf(h2, "h2", h2T, "h2T", 8)
                yield
                for g4 in range(4):
                    pt, pn = self.next_ps()
                    for gg in range(4):
                        fc = g4 * 4 + gg
                        for k in range(8):
                            fw.op("pe", lambda: nc.tensor.matmul(pt[:, gg * 128:(gg + 1) * 128], lhsT=wq[:, k, fc * 128:(fc + 1) * 128],
                                                                 rhs=h2T[:, k, :], start=(k == 0), stop=(k == 7)),
                                  reads=["wq", "h2T"], writes=[pn], same_ok=True)
                        yield
                    if g4 % 2 == 0:
                        fw.op("act", lambda: nc.scalar.copy(qT[:, g4 * 4:(g4 + 1) * 4, :], pt[:].rearrange("p (a b) -> p a b", a=4)),
                              reads=[pn], writes=["qT%d" % g4])
                    else:
                        fw.op("dve", lambda: V.tensor_copy(qT[:, g4 * 4:(g4 + 1) * 4, :], pt[:].rearrange("p (a b) -> p a b", a=4)),
                              reads=[pn], writes=["qT%d" % g4])
                    yield
                for g4 in range(4):
                    pt, pn = self.next_ps()
                    for gg in range(4):
                        fc = g4 * 4 + gg
                        fw.op("pe", lambda: nc.tensor.matmul(pt[:, gg * 128:(gg + 1) * 128], lhsT=qT[:, fc, :], rhs=kT[:, fc, :], start=True, stop=True),
                              reads=["qT%d" % g4, "kT"], writes=[pn], same_ok=True)
                    fw.op("act", lambda: nc.scalar.copy(s_sb[:, g4 * 4:(g4 + 1) * 4, :], pt[:].rearrange("p (a b) -> p a b", a=4)),
                          reads=[pn], writes=["s_sb%d" % g4, "oh"])
                    yield
                for g in range(16):
                    fw.op("dve", lambda: V.max(out=m1[:, g, 0:8], in_=s_sb[:, g, :]), reads=["s_sb%d" % (g // 4)], writes=["m1a%d" % g])
                    yield
                for g in range(16):
                    fw.op("dve", lambda: V.max_index(out=i1[:, g, 0:8], in_max=m1[:, g, 0:8], in_values=s_sb[:, g, :]),
                          reads=["s_sb%d" % (g // 4), "m1a%d" % g], writes=["i1a%d" % g])
                    yield
                for g in range(16):
                    fw.op("dve", lambda: V.match_replace(out=s_wk[:, g, :], in_to_replace=m1[:, g, 0:8], in_values=s_sb[:, g, :], imm_value=-1e30),
                          reads=["s_sb%d" % (g // 4), "m1a%d" % g, "oh2"], writes=["s_wk%d" % g])
                    yield
                for g in range(16):
                    fw.op("dve", lambda: V.max(out=m1[:, g, 8:16], in_=s_wk[:, g, :]), reads=["s_wk%d" % g], writes=["m1b%d" % g])
                    yield
                for g in range(16):
                    fw.op("dve", lambda: V.max_index(out=i1[:, g, 8:16], in_max=m1[:, g, 8:16], in_values=s_wk[:, g, :]),
                          reads=["s_wk%d" % g, "m1b%d" % g], writes=["i1b%d" % g])
                    yield
                allm1 = ["m1a%d" % g for g in range(16)] + ["m1b%d" % g for g in range(16)]
                alli1 = ["i1a%d" % g for g in range(16)] + ["i1b%d" % g for g in range(16)]
                m1v = m1[:].rearrange("p (h j) a -> p h j a", j=2)
                fw.op("dve", lambda: V.tensor_tensor(out=cand.rearrange("p h (a b) -> p h a b", a=16),
                                                     in0=m1v[:, :, 0, :].unsqueeze(3).broadcast_to([128, 8, 16, 16]),
                                                     in1=m1v[:, :, 1, :].unsqueeze(2).broadcast_to([128, 8, 16, 16]), op=ALU.add),
                      reads=allm1, writes=["cand"])
                yield
                for h in range(8):
                    fw.op("dve", lambda: V.max(out=m2[:, h, 0:8], in_=cand[:, h, :]), reads=["cand"], writes=["m2a%d" % h])
                    yield
                for h in range(8):
                    fw.op("dve", lambda: V.max_index(out=p2[:, h, 0:8], in_max=m2[:, h, 0:8], in_values=cand[:, h, :]),
                          reads=["cand", "m2a%d" % h], writes=["p2a%d" % h])
                    yield
                for h in range(8):
                    fw.op("dve", lambda: V.match_replace(out=cand_wk[:, h, :], in_to_replace=m2[:, h, 0:8], in_values=cand[:, h, :], imm_value=-1e30),
                          reads=["cand", "m2a%d" % h], writes=["cwk%d" % h])
                    yield
                for h in range(8):
                    fw.op("dve", lambda: V.max(out=m2[:, h, 8:16], in_=cand_wk[:, h, :]), reads=["cwk%d" % h], writes=["m2b%d" % h])
                    yield
                for h in range(8):
                    fw.op("dve", lambda: V.max_index(out=p2[:, h, 8:16], in_max=m2[:, h, 8:16], in_values=cand_wk[:, h, :]),
                          reads=["cwk%d" % h, "m2b%d" % h], writes=["p2b%d" % h])
                    yield
                allm2 = ["m2a%d" % h for h in range(8)] + ["m2b%d" % h for h in range(8)]
                allp2 = ["p2a%d" % h for h in range(8)] + ["p2b%d" % h for h in range(8)]
                fw.op("dve", lambda: V.tensor_tensor(out=ee[:], in0=m2[:], in1=m2[:, :, 0:1].broadcast_to([128, 8, 16]), op=ALU.subtract),
                      reads=allm2, writes=["ee"])
                fw.op("act", lambda: nc.scalar.activation(out=ee[:], in_=ee[:], func=AF.Exp), reads=["ee"], writes=["ee"])
                yield
                fw.op("dve", lambda: V.reduce_sum(out=zz[:], in_=ee[:], axis=AX.X), reads=["ee"], writes=["zz"])
                fw.op("dve", lambda: V.reciprocal(out=zz[:], in_=zz[:]), reads=["zz"], writes=["zz"])
                fw.op("dve", lambda: V.tensor_tensor(out=gg_[:], in0=ee[:], in1=zz[:].unsqueeze(2).broadcast_to([128, 8, 16]), op=ALU.mult),
                      reads=["ee", "zz"], writes=[gk])
                yield
                fw.op("dve", lambda: V.tensor_single_scalar(out=pa[:], in_=p2[:], scalar=4, op=ALU.logical_shift_right), reads=allp2, writes=["pa"])
                fw.op("dve", lambda: V.tensor_single_scalar(out=pbb[:], in_=p2[:], scalar=15, op=ALU.bitwise_and), reads=allp2, writes=["pbb"])
                yield
                fw.op("dve", lambda: V.tensor_copy(paf[:], pa[:]), reads=["pa"], writes=["paf"])
                fw.op("dve", lambda: V.tensor_copy(pbf[:], pbb[:]), reads=["pbb"], writes=["pbf"])
                fw.op("dve", lambda: V.tensor_copy(i1f[:], i1[:]), reads=alli1, writes=["i1f"])
                yield
                i1v = i1f[:].rearrange("p (h j) a -> p h j a", j=2)
                io_b = self.iota16[:].unsqueeze(1).unsqueeze(1).broadcast_to([128, 8, 16, 16])
                for (pf, pfn, jj, sel, seln, ohh, ohn, alias) in ((paf, "paf", 0, sel1, "sel1", oh, "oh", SSB), (pbf, "pbf", 1, sel2, "sel2", oh2, "oh2", SWK)):
                    fw.op("dve", lambda: V.tensor_tensor(out=ohh, in0=io_b, in1=pf[:].unsqueeze(3).broadcast_to([128, 8, 16, 16]), op=ALU.is_equal),
                          reads=["iota16", pfn], writes=[ohn] + alias)
                    yield
                    fw.op("dve", lambda: V.tensor_tensor(out=ohh, in0=ohh, in1=i1v[:, :, jj, :].unsqueeze(2).broadcast_to([128, 8, 16, 16]), op=ALU.mult),
                          reads=[ohn, "i1f"], writes=[ohn] + alias)
                    yield
                    fw.op("dve", lambda: V.reduce_sum(out=sel[:], in_=ohh, axis=AX.X), reads=[ohn], writes=[seln])
                    yield
                fw.op("dve", lambda: V.scalar_tensor_tensor(out=idxf[:], in0=sel1[:].rearrange("p h k -> p (h k)"), scalar=128.0,
                                                            in1=sel2[:].rearrange("p h k -> p (h k)"), op0=ALU.mult, op1=ALU.add),
                      reads=["sel1", "sel2"], writes=["idxf"])
                fw.op("dve", lambda: V.tensor_copy(idxu[:], idxf[:]), reads=["idxf"], writes=[ik])
                yield

            def drain(gen, n=None):
                if gen is None:
                    return
                k = 0
                for _ in gen:
                    k += 1
                    if n is not None and k >= n:
                        return

            load_mod3(0)
            drain(front(0))
            for t in range(n_tiles):
                b = t % 2
                xt, h2b, idxu = xts[b], h2bs[b], idxus[b]
                xk, hk, ik, gk = "p_xt%d" % b, "h2b_%d" % b, "idxu%d" % b, "gg%d" % b
                ggf = ggfs[b]
                dst, okey = dst_fn(t)
                nxt = front(t + 1) if (t + 1 < n_tiles and t + 1 != n_x) else None
                def stage_a(gi):
                    for jj in range(GJ):
                        j = gi * GJ + jj
                        r = j % NSLOT
                        pr = j % 4
                        fw.dma("pool", lambda: nc.gpsimd.indirect_dma_start(out=cb[r][:], out_offset=None, in_=uv_bf[:, :],
                                                                            in_offset=bass.IndirectOffsetOnAxis(ap=idxu[:, j:j + 1], axis=0)),
                               cbs[r], reads=[ik], writes=["cb%d" % r])
                        fw.op("dve", lambda: V.tensor_tensor(out=prod[pr][:], in0=cb[r][:, 0:D], in1=h2b[:], op=ALU.mult),
                              reads=["cb%d" % r, hk], writes=["prod%d" % pr])
                        fw.op("act", lambda: nc.scalar.activation(out=prod[pr][:], in_=prod[pr][:], func=AF.Copy, accum_out=apre[:, j:j + 1]),
                              reads=["prod%d" % pr], writes=["prod%d" % pr, "apre%d" % (gi % 4)], same_ok=True)
                    gs = slice(gi * GJ, (gi + 1) * GJ)
                    fw.op("act", lambda: nc.scalar.activation(out=ge[:, gs], in_=apre[:, gs], func=AF.Gelu), reads=["apre%d" % (gi % 4)], writes=["ge%d" % (gi % 4)])

                def stage_b(gi):
                    gs = slice(gi * GJ, (gi + 1) * GJ)
                    fw.op("dve", lambda: V.tensor_tensor(out=wgt[:, gs], in0=ge[:, gs], in1=ggf[:, gs], op=ALU.mult),
                          reads=["ge%d" % (gi % 4), gk], writes=["wgt%d" % (gi % 4)])
                    dg = dgs[gi % 2]
                    dk = "dg%d" % (gi % 2)
                    fw.op("dve", lambda: V.tensor_tensor(out=dg[:], in0=wgt[:, gs].unsqueeze(2).broadcast_to([128, GJ, 128]),
                                                         in1=self.identb[:].unsqueeze(1).broadcast_to([128, GJ, 128]), op=ALU.mult),
                          reads=["wgt%d" % (gi % 4), "identb"], writes=[dk])
                    for jj in range(GJ):
                        j = gi * GJ + jj
                        r = j % NSLOT
                        for n in range(2):
                            fw.op("pe", lambda: nc.tensor.matmul(accp[n][0][:, :], lhsT=dg[:, jj, :], rhs=cb[r][:, D + n * 512:D + (n + 1) * 512],
                                                                 start=(j == 0), stop=(j == 127)),
                                  reads=[dk, "cb%d" % r], writes=[accp[n][1]], same_ok=True)

                NG = 128 // GJ
                for gi in range(NG):
                    stage_a(gi)
                    if gi >= 1:
                        stage_b(gi - 1)
                    drain(nxt, 4)
                stage_b(NG - 1)
                drain(nxt)
                self.post_norm(xt, xk, [accp[0][0][:, :], accp[1][0][:, :]], ["pb6", "pb7"], mod, 2 * D, lng[:], lnb[:],
                               tmp, "h2", xo, "p_xo", (stats, mv, sd))
                fw.dma("sp", lambda: nc.sync.dma_start(out=dst, in_=xo[:]), self.dsem("xo"), reads=["p_xo"], writes=[okey])
                if t + 1 < n_tiles and t + 1 == n_x:
                    load_mod3(1)
                    drain(front(t + 1))
            self.ps_hi = 8
            fw.barrier()

    def phase_pool(self, a, modrows, load_x, n_x, halo, ctx, x1s, ctx_needed):
        nc, fw = self.nc, self.fw
        V = nc.vector
        with ExitStack() as es:
            mod = self.sb(es, "mod", [128, 6 * D], F32)
            lng = self.sb(es, "lng", [128, D], F32)
            lnb = self.sb(es, "lnb", [128, D], F32)
            w_in = self.sb(es, "pw_in", [128, 8, D], BF16)
            w_grp = self.sb(es, "pw_grp", [128, 8, 256], BF16)
            w_out = self.sb(es, "pw_out", [128, 8, D], BF16)
            psc = self.sb(es, "pscale", [128, 8], F32)
            band = self.sb(es, "band", [128, 4, 9, 128], F32)
            zb = [self.sb(es, "zb%d" % i, [128, D], F32) for i in range(4)]
            zh = self.sb(es, "zh", [16, D], F32)
            xts = [self.sb(es, "m_xt%d" % i, [128, D], F32) for i in range(2)]
            hh = self.sb(es, "m_h", [128, D], F32)
            hT = self.sb(es, "m_hT", [128, 8, 128], BF16)
            hT16 = self.sb(es, "m_hT16", [128, 8, 16], BF16)
            pooledT = self.sb(es, "pooledT", [128, 8, 128], BF16)
            y1T = self.sb(es, "y1T", [128, 8, 128], BF16)
            tmp = self.sb(es, "m_tmp", [128, D], F32)
            xo = self.sb(es, "m_xo", [128, D], F32)
            stats = self.sb(es, "m_stats", [128, 2, 6], F32)
            mv = self.sb(es, "m_mv", [128, 2], F32)
            sd = self.sb(es, "m_sd", [128, 1], F32)
            G = nc.gpsimd
            fw.dma("pool", lambda: G.dma_start(out=w_in[:], in_=a["pw_in"].rearrange("(k p) n -> p k n", p=128)), self.dsem("wload"), sync=True, writes=["pw_in"])
            fw.dma("pool", lambda: G.dma_start(out=w_out[:], in_=a["pw_out"].rearrange("(k p) n -> p k n", p=128)), self.dsem("wload"), sync=True, writes=["pw_out"])
            fw.dma("pool", lambda: G.dma_start(out=w_grp[:].rearrange("p (g c) n -> p g c n", g=4), in_=a["pw_grp"].rearrange("g (c p) n -> p g c n", p=128)),
                   self.dsem("wload"), sync=True, writes=["pw_grp"])
            fw.dma("sp", lambda: nc.sync.dma_start(out=psc[:], in_=a["pscale"][:, :]), self.dsem("misc"), sync=True, writes=["psc"])
            fw.dma("sp", lambda: nc.sync.dma_start(out=band[:], in_=a["band"][:, :, :, :]), self.dsem("misc"), sync=True, writes=["band"])
            fw.dma("sp", lambda: nc.sync.dma_start(out=lng[:], in_=a["lng"][0:1, :].broadcast_to([128, D])), self.dsem("misc"), sync=True, writes=["lnp"])
            fw.dma("sp", lambda: nc.sync.dma_start(out=lnb[:], in_=a["lnb"][0:1, :].broadcast_to([128, D])), self.dsem("misc"), sync=True, writes=["lnp"])

            def compute_z(loader, zt, zkey, xt, xkey, rows=128):
                loader(xt, xkey)
                self.modulate(xt, xkey, mod, 0, D, hh, "m_h", rows=rows)
                if rows == 128:
                    self.transpose_bf(hh, "m_h", hT, "m_hT", 8)
                    lt = hT
                    lk = "m_hT"
                else:
                    self.transpose_bf(hh, "m_h", hT16, "m_hT16", 8, rows=rows)
                    lt = hT16
                    lk = "m_hT16"
                for n in range(2):
                    pt, pn = self.next_ps()
                    for k in range(8):
                        fw.op("pe", lambda: nc.tensor.matmul(pt[0:rows, :], lhsT=lt[:, k, :], rhs=w_in[:, k, n * 512:(n + 1) * 512], start=(k == 0), stop=(k == 7)),
                              reads=[lk, "pw_in"], writes=[pn], same_ok=True)
                    fw.op("act", lambda: nc.scalar.copy(zt[0:rows, n * 512:(n + 1) * 512], pt[0:rows, :]), reads=[pn], writes=[zkey])

            def mix_tile(zprev, zcur, znext, vcen, zhv, xt, xkey, dst):
                for half in range(2):
                    pt, pn = self.next_ps()
                    for j in range(4):
                        cc = half * 4 + j
                        g = cc // 2
                        terms = []
                        if zprev is not None:
                            terms.append((zprev[0][:, cc * 128:(cc + 1) * 128], band[:, g, 1, :], zprev[1]))
                        terms.append((zcur[0][:, cc * 128:(cc + 1) * 128], band[:, g, vcen, :], zcur[1]))
                        if znext is not None:
                            terms.append((znext[0][:, cc * 128:(cc + 1) * 128], band[:, g, 2, :], znext[1]))
                        if zhv is not None:
                            terms.append((zh[0:16, cc * 128:(cc + 1) * 128], band[0:16, g, zhv, :], "zh"))
                        for ti, (l, r, lk) in enumerate(terms):
                            fw.op("pe", lambda: nc.tensor.matmul(pt[:, j * 128:(j + 1) * 128], lhsT=l, rhs=r, start=(ti == 0), stop=(ti == len(terms) - 1)),
                                  reads=[lk, "band"], writes=[pn], same_ok=True)
                    fw.op("act", lambda: nc.scalar.copy(pooledT[:, half * 4:(half + 1) * 4, :], pt[:].rearrange("p (a b) -> p a b", a=4)),
                          reads=[pn], writes=["pooledT"])
                for half in range(2):
                    pt, pn = self.next_ps()
                    for j in range(4):
                        dc = half * 4 + j
                        g = dc // 2
                        m = dc % 2
                        for c in range(2):
                            fw.op("pe", lambda: nc.tensor.matmul(pt[:, j * 128:(j + 1) * 128], lhsT=w_grp[:, g * 2 + c, m * 128:(m + 1) * 128],
                                                                 rhs=pooledT[:, g * 2 + c, :], start=(c == 0), stop=(c == 1)),
                                  reads=["pw_grp", "pooledT"], writes=[pn], same_ok=True)
                    fw.op("dve", lambda: V.tensor_tensor(out=y1T[:, half * 4:(half + 1) * 4, :], in0=pt[:].rearrange("p (a b) -> p a b", a=4),
                                                         in1=psc[:, half * 4:(half + 1) * 4].unsqueeze(2).broadcast_to([128, 4, 128]), op=ALU.mult),
                          reads=[pn, "psc"], writes=["y1T"])
                yp = []
                for n in range(2):
                    pt, pn = self.next_ps()
                    for k in range(8):
                        fw.op("pe", lambda: nc.tensor.matmul(pt[:, :], lhsT=y1T[:, k, :], rhs=w_out[:, k, n * 512:(n + 1) * 512], start=(k == 0), stop=(k == 7)),
                              reads=["y1T", "pw_out"], writes=[pn], same_ok=True)
                    yp.append((pt, pn))
                self.post_norm(xt, xkey, [yp[0][0][:, :], yp[1][0][:, :]], [yp[0][1], yp[1][1]], mod, 2 * D, lng[:], lnb[:],
                               tmp, "m_tmp", xo, "m_xo", (stats, mv, sd))
                fw.dma("sp", lambda: nc.sync.dma_start(out=dst, in_=xo[:]), self.dsem("xo"), reads=["m_xo"], writes=["x1s"])

            def plain(src, srckey, rows=128):
                return lambda xt, xkey: fw.dma("sp", lambda: nc.sync.dma_start(out=xt[0:rows, :], in_=src), self.dsem(xkey), reads=[srckey], writes=[xkey])

            self.load_mod(mod, modrows, 0)
            if halo is not None:
                compute_z(plain(halo, "xsrc", 16), zh, "zh", xts[0], "m_xt0", rows=16)
            compute_z(lambda xt, xkey: load_x(0, xt, xkey), zb[0], "zb0", xts[0], "m_xt0")
            for T in range(n_x):
                if T + 1 < n_x:
                    compute_z(lambda xt, xkey: load_x(T + 1, xt, xkey), zb[(T + 1) % 4], "zb%d" % ((T + 1) % 4), xts[(T + 1) % 2], "m_xt%d" % ((T + 1) % 2))
                zprev = (zb[(T - 1) % 4], "zb%d" % ((T - 1) % 4)) if T > 0 else None
                znext = (zb[(T + 1) % 4], "zb%d" % ((T + 1) % 4)) if T + 1 < n_x else None
                vcen = 3 if T == 0 else (4 if T == n_x - 1 else 0)
                zhv = None
                if halo is not None:
                    zhv = 7 if T == 0 else (8 if T == n_x - 1 else None)
                mix_tile(zprev, (zb[T % 4], "zb%d" % (T % 4)), znext, vcen, zhv, xts[T % 2], "m_xt%d" % (T % 2), x1s[T * 128:(T + 1) * 128, :])
            if ctx_needed:
                self.load_mod(mod, modrows, 1)
                compute_z(plain(ctx[0:128, :], "csrc"), zb[0], "zb0", xts[0], "m_xt0")
                compute_z(plain(ctx[128:256, :], "csrc"), zb[1], "zb1", xts[1], "m_xt1")
                mix_tile(None, (zb[0], "zb0"), (zb[1], "zb1"), 5, None, xts[0], "m_xt0", x1s[n_x * 128:n_x * 128 + 128, :])
                mix_tile((zb[0], "zb0"), (zb[1], "zb1"), None, 6, None, xts[1], "m_xt1", x1s[n_x * 128 + 128:n_x * 128 + 256, :])
            fw.barrier()

    def phase_mla(self, a, modrows, load_q, n_xq, xkeys, ctx, x1s, attn_d, cq_d, ctx_needed):
        nc, fw = self.nc, self.fw
        V = nc.vector
        G = nc.gpsimd
        nqt = n_xq + (2 if ctx_needed else 0)
        NQ = nqt * 128
        QB = 2048
        SCALE = 192.0 ** -0.5

        def plain(src, srckey):
            return lambda xt, xkey: fw.dma("sp", lambda: nc.sync.dma_start(out=xt[:], in_=src), self.dsem(xkey), reads=[srckey], writes=[xkey])

        with ExitStack() as es_outer:
            latT = self.sb(es_outer, "latT", [128, 3, NKT * 128], BF16)
            with ExitStack() as es:
                mod = self.sb(es, "mod", [128, 6 * D], F32)
                w_in = self.sb(es, "w_in", [128, 8, 704], BF16)
                qng = self.sb(es, "qng", [128, 384], F32)
                kvng = self.sb(es, "kvng", [128, 256], F32)
                xts = [self.sb(es, "a_xt%d" % i, [128, D], F32) for i in range(2)]
                rks = [self.sb(es, "a_rk%d" % i, [128, 64], F32) for i in range(2)]
                cqs = [self.sb(es, "a_cqs%d" % i, [128, 3, 128], BF16) for i in range(2)]
                hh = self.sb(es, "a_h", [128, D], F32)
                hT = self.sb(es, "a_hT", [128, 8, 128], BF16)
                junk = self.sb(es, "a_junk", [128, 384], F32)
                ss = self.sb(es, "a_ss", [128, 1], F32)
                lat = self.sb(es, "a_lat", [128, 384], F32)
                rA = self.sb(es, "a_rA", [128, 2, 32], F32)
                rB = self.sb(es, "a_rB", [128, 2, 32], F32)
                fw.dma("pool", lambda: G.dma_start(out=w_in[:], in_=a["w_in"].rearrange("(k p) n -> p k n", p=128)), self.dsem("wload"), sync=True, writes=["w_in"])
                fw.dma("sp", lambda: nc.sync.dma_start(out=qng[:], in_=a["qng"][0:1, :].broadcast_to([128, 384])), self.dsem("misc"), sync=True, writes=["qng"])
                fw.dma("sp", lambda: nc.sync.dma_start(out=kvng[:], in_=a["kvng"][0:1, :].broadcast_to([128, 256])), self.dsem("misc"), sync=True, writes=["kvng"])

                def front(loader, i):
                    xt = xts[i % 2]
                    xk = "a_xt%d" % (i % 2)
                    loader(xt, xk)
                    self.modulate(xt, xk, mod, 0, D, hh, "a_h")
                    self.transpose_bf(hh, "a_h", hT, "a_hT", 8)

                def rms(pt, pn, n, gtile, gkey):
                    fw.op("act", lambda: nc.scalar.activation(out=junk[:, 0:n], in_=pt[:, 0:n], func=AF.Square, accum_out=ss[:, 0:1]),
                          reads=[pn], writes=["a_junk", "a_ss"])
                    fw.op("act", lambda: nc.scalar.activation(out=ss[:], in_=ss[:], func=AF.Sqrt, bias=self.eps_rms[:, 0:1], scale=1.0 / n),
                          reads=["a_ss", "eps"], writes=["a_ss"])
                    fw.op("dve", lambda: V.reciprocal(out=ss[:], in_=ss[:]), reads=["a_ss"], writes=["a_ss"])
                    fw.op("dve", lambda: V.scalar_tensor_tensor(out=lat[:, 0:n], in0=pt[:, 0:n], scalar=ss[:, 0:1], in1=gtile[:, 0:n], op0=ALU.mult, op1=ALU.mult),
                          reads=[pn, "a_ss", gkey], writes=["a_lat"])

                for phase_src in (0, 1):
                    self.load_mod(mod, modrows, phase_src)
                    if phase_src == 0:
                        ktiles = [(plain(xkeys[kt * 128:(kt + 1) * 128, :], "xsrc"), kt) for kt in range(SEQ // 128)]
                        qtiles = [((lambda xt, xk, t=t: load_q(t, xt, xk)), t) for t in range(n_xq)]
                    else:
                        ktiles = [(plain(ctx[j * 128:(j + 1) * 128, :], "csrc"), SEQ // 128 + j) for j in range(2)]
                        qtiles = [(plain(ctx[j * 128:(j + 1) * 128, :], "csrc"), n_xq + j) for j in range(2)] if ctx_needed else []
                    for i, (loader, kt) in enumerate(ktiles):
                        rk = rks[i % 2]
                        rkk = "a_rk%d" % (i % 2)
                        fw.dma("sp", lambda: nc.sync.dma_start(out=rk[:], in_=a["ropek"][kt * 128:(kt + 1) * 128, :]), self.dsem(rkk), writes=[rkk])
                        front(loader, i)
                        pt, pn = self.next_ps()
                        for k in range(8):
                            fw.op("pe", lambda: nc.tensor.matmul(pt[:, 0:320], lhsT=hT[:, k, :], rhs=w_in[:, k, 384:704], start=(k == 0), stop=(k == 7)),
                                  reads=["a_hT", "w_in"], writes=[pn], same_ok=True)
                        rms(pt, pn, 256, kvng, "kvng")
                        zr = pt[:, 256:320].rearrange("p (a b) -> p a b", a=2)
                        fw.op("dve", lambda: V.tensor_tensor(out=rA[:], in0=zr, in1=rk[:, 0:32].unsqueeze(1).broadcast_to([128, 2, 32]), op=ALU.mult),
                              reads=[pn, rkk], writes=["a_rA"])
                        fw.op("dve", lambda: V.tensor_tensor(out=rB[:], in0=zr, in1=rk[:, 32:64].unsqueeze(1).broadcast_to([128, 2, 32]), op=ALU.mult),
                              reads=[pn, rkk], writes=["a_rB"])
                        fw.op("dve", lambda: V.tensor_tensor(out=lat[:, 256:288], in0=rA[:, 0, :], in1=rB[:, 1, :], op=ALU.subtract),
                              reads=["a_rA", "a_rB"], writes=["a_lat"])
                        fw.op("dve", lambda: V.tensor_tensor(out=lat[:, 288:320], in0=rB[:, 0, :], in1=rA[:, 1, :], op=ALU.add),
                              reads=["a_rA", "a_rB"], writes=["a_lat"])
                        pt2, pn2 = self.next_ps()
                        for c in range(2):
                            fw.op("pe", lambda: nc.tensor.transpose(pt2[:, c * 128:(c + 1) * 128], lat[:, c * 128:(c + 1) * 128], self.ident[:]),
                                  reads=["a_lat", "ident"], writes=[pn2], same_ok=True)
                        fw.op("pe", lambda: nc.tensor.transpose(pt2[0:64, 256:384], lat[:, 256:320], self.ident[:]),
                              reads=["a_lat", "ident"], writes=[pn2], same_ok=True)
                        fw.op("act", lambda: nc.scalar.copy(latT[:, 0:2, kt * 128:(kt + 1) * 128], pt2[:, 0:256].rearrange("p (a b) -> p a b", a=2)),
                              reads=[pn2], writes=["latT"])
                        fw.op("act", lambda: nc.scalar.copy(latT[0:64, 2, kt * 128:(kt + 1) * 128], pt2[0:64, 256:384]),
                              reads=[pn2], writes=["latT"])
                    for i, (loader, qt) in enumerate(qtiles):
                        front(loader, i)
                        pt, pn = self.next_ps()
                        for k in range(8):
                            fw.op("pe", lambda: nc.tensor.matmul(pt[:, 0:384], lhsT=hT[:, k, :], rhs=w_in[:, k, 0:384], start=(k == 0), stop=(k == 7)),
                                  reads=["a_hT", "w_in"], writes=[pn], same_ok=True)
                        rms(pt, pn, 384, qng, "qng")
                        pt2, pn2 = self.next_ps()
                        for c in range(3):
                            fw.op("pe", lambda: nc.tensor.transpose(pt2[:, c * 128:(c + 1) * 128], lat[:, c * 128:(c + 1) * 128], self.ident[:]),
                                  reads=["a_lat", "ident"], writes=[pn2], same_ok=True)
                        cs = cqs[i % 2]
                        ck = "a_cqs%d" % (i % 2)
                        fw.op("act", lambda: nc.scalar.copy(cs[:], pt2[:, 0:384].rearrange("p (a b) -> p a b", a=3)), reads=[pn2], writes=[ck])
                        fw.dma("sp", lambda: nc.sync.dma_start(out=cq_d[:, :, qt * 128:(qt + 1) * 128], in_=cs[:]), self.dsem(ck), reads=[ck], writes=["cq_d"])
                fw.barrier()
            with ExitStack() as es:
                w_uq = self.sb(es, "w_uq", [128, 3, 1536], BF16)
                w_rot = self.sb(es, "w_rot", [128, 3, 512], BF16)
                w_ukv = self.sb(es, "w_ukv", [128, 2, 2048], BF16)
                cqT = self.sb(es, "cqT", [128, 3, QB], BF16)
                cosT = self.sb(es, "cosT", [64, QB], F32)
                sinT = self.sb(es, "sinT", [64, QB], F32)
                KnT = self.sb(es, "KnT", [128, NKT * 128], BF16)
                Vs = self.sb(es, "Vs", [128, NKT, 128], BF16)
                qnT = self.sb(es, "qnT", [128, QB], BF16)
                qrT = self.sb(es, "qrT", [64, QB], BF16)
                r1 = self.sb(es, "r1", [64, 512], F32)
                r2 = self.sb(es, "r2", [64, 512], F32)
                PT = [self.sb(es, "PT%d" % i, [128, 512], BF16) for i in range(4)]
                s_ring = [0, 1, 5]
                s_cnt = [0]
                rec = self.sb(es, "rec", [128, 512], F32)
                ao = [self.sb(es, "ao%d" % i, [128, 512], BF16) for i in range(2)]
                Lacc = [self.sb(es, "Lacc%d" % i, [128, 512], F32) for i in range(2)]
                ones_f = self.sb(es, "ones_f", [128, 128], F32)
                fw.op("dve", lambda: V.memset(ones_f[:], 1.0), writes=["ones_f"])
                fw.dma("pool", lambda: G.dma_start(out=w_uq[:], in_=a["w_uq"].rearrange("(k p) n -> p k n", p=128)), self.dsem("wload"), sync=True, writes=["w_uq"])
                fw.dma("pool", lambda: G.dma_start(out=w_ukv[:], in_=a["w_ukv"].rearrange("(k p) n -> p k n", p=128)), self.dsem("wload"), sync=True, writes=["w_ukv"])
                for c in range(3):
                    rv = w_uq[:, c, :].rearrange("p (h f) -> p h f", h=8)
                    ov = w_rot[:, c, :].rearrange("p (h f) -> p h f", h=8)
                    fw.op("dve", lambda: V.tensor_scalar(ov[:, :, 0:32], rv[:, :, 160:192], -1.0, None, ALU.mult), reads=["w_uq"], writes=["w_rot"])
                    fw.op("dve", lambda: V.tensor_copy(ov[:, :, 32:64], rv[:, :, 128:160]), reads=["w_uq"], writes=["w_rot"])
                pbi_evac = [0]

                def evac(dst_ap, src_ap, pn, dkey):
                    pbi_evac[0] += 1
                    if pbi_evac[0] % 2 == 0:
                        fw.op("act", lambda: nc.scalar.copy(dst_ap, src_ap), reads=[pn], writes=[dkey])
                    else:
                        fw.op("dve", lambda: V.tensor_copy(dst_ap, src_ap), reads=[pn], writes=[dkey])

                blocks = []
                for q0 in range(0, n_xq * 128, QB):
                    blocks.append((q0, min(QB, n_xq * 128 - q0), 0, NKT))
                if ctx_needed:
                    blocks.append((n_xq * 128, CTX, SEQ // 128, NKT))
                ci = 0
                for (b0, bn, kt0, kt1) in blocks:
                    fw.dma("sp", lambda: nc.sync.dma_start(out=cqT[:, :, 0:bn], in_=cq_d[:, :, b0:b0 + bn]), self.dsem("cqb"), sync=True, reads=["cq_d"], writes=["cqT"])
                    fw.dma("sp", lambda: nc.sync.dma_start(out=cosT[:, 0:bn], in_=a["ropeq"][:, 0, b0:b0 + bn]), self.dsem("cqb"), sync=True, writes=["ropeq"])
                    fw.dma("sp", lambda: nc.sync.dma_start(out=sinT[:, 0:bn], in_=a["ropeq"][:, 1, b0:b0 + bn]), self.dsem("cqb"), sync=True, writes=["ropeq"])
                    for h in range(8):
                        for kc in range(kt0 * 128, kt1 * 128, 512):
                            n = min(512, kt1 * 128 - kc)
                            pt, pn = self.next_ps(6, 8)
                            for c in range(2):
                                fw.op("pe", lambda: nc.tensor.matmul(pt[:, 0:n], lhsT=w_ukv[:, c, h * 256:h * 256 + 128], rhs=latT[:, c, kc:kc + n], start=(c == 0), stop=(c == 1)),
                                      reads=["w_ukv", "latT"], writes=[pn], same_ok=True)
                            evac(KnT[:, kc:kc + n], pt[:, 0:n], pn, "KnT")
                        for k4 in range(kt0, kt1, 4):
                            n = min(4, kt1 - k4)
                            pt, pn = self.next_ps(6, 8)
                            for j in range(n):
                                kt = k4 + j
                                for c in range(2):
                                    fw.op("pe", lambda: nc.tensor.matmul(pt[:, j * 128:(j + 1) * 128], lhsT=latT[:, c, kt * 128:(kt + 1) * 128],
                                                                         rhs=w_ukv[:, c, h * 256 + 128:h * 256 + 256], start=(c == 0), stop=(c == 1)),
                                          reads=["w_ukv", "latT"], writes=[pn], same_ok=True)
                            evac(Vs[:, k4:k4 + n, :], pt[:, 0:n * 128].rearrange("p (a b) -> p a b", a=n), pn, "Vs")
                        for q0 in range(0, bn, 512):
                            n = min(512, bn - q0)
                            pt, pn = self.next_ps(6, 8)
                            for c in range(3):
                                fw.op("pe", lambda: nc.tensor.matmul(pt[:, 0:n], lhsT=w_uq[:, c, h * 192:h * 192 + 128], rhs=cqT[:, c, q0:q0 + n], start=(c == 0), stop=(c == 2)),
                                      reads=["w_uq", "cqT"], writes=[pn], same_ok=True)
                            evac(qnT[:, q0:q0 + n], pt[:, 0:n], pn, "qnT")
                            pt1, pn1 = self.next_ps(6, 8)
                            for c in range(3):
                                fw.op("pe", lambda: nc.tensor.matmul(pt1[0:64, 0:n], lhsT=w_uq[:, c, h * 192 + 128:h * 192 + 192], rhs=cqT[:, c, q0:q0 + n], start=(c == 0), stop=(c == 2)),
                                      reads=["w_uq", "cqT"], writes=[pn1], same_ok=True)
                            fw.op("dve", lambda: V.tensor_tensor(out=r1[:, 0:n], in0=pt1[0:64, 0:n], in1=cosT[:, q0:q0 + n], op=ALU.mult), reads=[pn1, "ropeq"], writes=["r1"])
                            pt2, pn2 = self.next_ps(6, 8)
                            for c in range(3):
                                fw.op("pe", lambda: nc.tensor.matmul(pt2[0:64, 0:n], lhsT=w_rot[:, c, h * 64:(h + 1) * 64], rhs=cqT[:, c, q0:q0 + n], start=(c == 0), stop=(c == 2)),
                                      reads=["w_rot", "cqT"], writes=[pn2], same_ok=True)
                            fw.op("dve", lambda: V.tensor_tensor(out=r2[:, 0:n], in0=pt2[0:64, 0:n], in1=sinT[:, q0:q0 + n], op=ALU.mult), reads=[pn2, "ropeq"], writes=["r2"])
                            fw.op("dve", lambda: V.tensor_tensor(out=qrT[:, q0:q0 + n], in0=r1[:, 0:n], in1=r2[:, 0:n], op=ALU.add), reads=["r1", "r2"], writes=["qrT"])
                        for q0 in range(0, bn, 512):
                            n = min(512, bn - q0)
                            ci += 1
                            O, On = self.pb[2 + ci % 2], "pb%d" % (2 + ci % 2)
                            L, Ln = self.pb[4], "pb4"
                            def emit_S(kt):
                                bi = s_ring[s_cnt[0] % 3]
                                s_cnt[0] += 1
                                S, Sn = self.pb[bi], "pb%d" % bi
                                fw.op("pe", lambda: nc.tensor.matmul(S[:, 0:n], lhsT=KnT[:, kt * 128:(kt + 1) * 128], rhs=qnT[:, q0:q0 + n], start=True, stop=False),
                                      reads=["KnT", "qnT"], writes=[Sn], same_ok=True)
                                fw.op("pe", lambda: nc.tensor.matmul(S[:, 0:n], lhsT=latT[0:64, 2, kt * 128:(kt + 1) * 128], rhs=qrT[:, q0:q0 + n], start=False, stop=True),
                                      reads=["latT", "qrT"], writes=[Sn], same_ok=True)
                                return S, Sn

                            pend = [emit_S(kt0), emit_S(kt0 + 1)]
                            for kt in range(kt0, kt1):
                                S, Sn = pend.pop(0)
                                if kt + 2 < kt1:
                                    pend.append(emit_S(kt + 2))
                                P = PT[kt % 4]
                                Pn = "PT%d" % (kt % 4)
                                fw.op("act", lambda: nc.scalar.activation(out=P[:, 0:n], in_=S[:, 0:n], func=AF.Exp, scale=SCALE), reads=[Sn], writes=[Pn])
                                fw.op("pe", lambda: nc.tensor.matmul(O[:, 0:n], lhsT=Vs[:, kt, :], rhs=P[:, 0:n], start=(kt == kt0), stop=(kt == kt1 - 1)),
                                      reads=["Vs", Pn], writes=[On], same_ok=True)
                                e = (kt - kt0) % 2
                                la = Lacc[e]
                                lk = "Lacc%d" % e
                                if e == 0:
                                    if kt - kt0 < 2:
                                        fw.op("dve", lambda: V.tensor_copy(la[:, 0:n], P[:, 0:n]), reads=[Pn], writes=[lk])
                                    else:
                                        fw.op("dve", lambda: V.tensor_tensor(out=la[:, 0:n], in0=la[:, 0:n], in1=P[:, 0:n], op=ALU.add), reads=[Pn, lk], writes=[lk])
                                else:
                                    if kt - kt0 < 2:
                                        fw.op("pool", lambda: G.tensor_copy(la[:, 0:n], P[:, 0:n]), reads=[Pn], writes=[lk])
                                    else:
                                        fw.op("pool", lambda: G.tensor_tensor(out=la[:, 0:n], in0=la[:, 0:n], in1=P[:, 0:n], op=ALU.add), reads=[Pn, lk], writes=[lk])
                            for e in range(2):
                                fw.op("pe", lambda: nc.tensor.matmul(L[:, 0:n], lhsT=ones_f[:, :], rhs=Lacc[e][:, 0:n], start=(e == 0), stop=(e == 1)),
                                      reads=["ones_f", "Lacc%d" % e], writes=[Ln], same_ok=True)
                            fw.op("dve", lambda: V.reciprocal(out=rec[:, 0:n], in_=L[:, 0:n]), reads=[Ln], writes=["rec"])
                            aot = ao[ci % 2]
                            aok = "ao%d" % (ci % 2)
                            fw.op("dve", lambda: V.tensor_tensor(out=aot[:, 0:n], in0=O[:, 0:n], in1=rec[:, 0:n], op=ALU.mult), reads=[On, "rec"], writes=[aok])
                            fw.dma("sp", lambda: nc.sync.dma_start(out=attn_d[h, :, b0 + q0:b0 + q0 + n], in_=aot[:, 0:n]), self.dsem(aok), reads=[aok], writes=["attn_d"])
                fw.barrier()
        with ExitStack() as es:
            mod = self.sb(es, "mod", [128, 6 * D], F32)
            lng = self.sb(es, "lng", [128, D], F32)
            lnb = self.sb(es, "lnb", [128, D], F32)
            w_o = self.sb(es, "w_o", [128, 8, D], BF16)
            xts = [self.sb(es, "c_xt%d" % i, [128, D], F32) for i in range(2)]
            ats = [self.sb(es, "c_at%d" % i, [128, 8, 128], BF16) for i in range(2)]
            tmp = self.sb(es, "c_tmp", [128, D], F32)
            xo = self.sb(es, "c_xo", [128, D], F32)
            stats = self.sb(es, "c_stats", [128, 2, 6], F32)
            mv = self.sb(es, "c_mv", [128, 2], F32)
            sd = self.sb(es, "c_sd", [128, 1], F32)
            fw.dma("pool", lambda: G.dma_start(out=w_o[:], in_=a["w_o"].rearrange("(k p) n -> p k n", p=128)), self.dsem("wload"), sync=True, writes=["w_o"])
            fw.dma("sp", lambda: nc.sync.dma_start(out=lng[:], in_=a["lng"][0:1, :].broadcast_to([128, D])), self.dsem("misc"), sync=True, writes=["lnp"])
            fw.dma("sp", lambda: nc.sync.dma_start(out=lnb[:], in_=a["lnb"][0:1, :].broadcast_to([128, D])), self.dsem("misc"), sync=True, writes=["lnp"])
            for t in range(nqt):
                if t == 0:
                    self.load_mod(mod, modrows, 0)
                if t == n_xq:
                    self.load_mod(mod, modrows, 1)
                xt = xts[t % 2]
                xk = "c_xt%d" % (t % 2)
                at = ats[t % 2]
                ak = "c_at%d" % (t % 2)
                if t < n_xq:
                    load_q(t, xt, xk)
                else:
                    plain(ctx[(t - n_xq) * 128:(t - n_xq + 1) * 128, :], "csrc")(xt, xk)
                fw.dma("sp", lambda: nc.sync.dma_start(out=at[:], in_=attn_d[:, :, t * 128:(t + 1) * 128].rearrange("h p t -> p h t")),
                       self.dsem(ak), reads=["attn_d"], writes=[ak])
                yp = []
                for n in range(2):
                    pt, pn = self.next_ps()
                    for h in range(8):
                        fw.op("pe", lambda: nc.tensor.matmul(pt[:, :], lhsT=at[:, h, :], rhs=w_o[:, h, n * 512:(n + 1) * 512], start=(h == 0), stop=(h == 7)),
                              reads=[ak, "w_o"], writes=[pn], same_ok=True)
                    yp.append((pt, pn))
                self.post_norm(xt, xk, [yp[0][0][:, :], yp[1][0][:, :]], [yp[0][1], yp[1][1]], mod, 2 * D, lng[:], lnb[:],
                               tmp, "c_tmp", xo, "c_xo", (stats, mv, sd))
                fw.dma("sp", lambda: nc.sync.dma_start(out=x1s[t * 128:(t + 1) * 128, :], in_=xo[:]), self.dsem("xo"), reads=["c_xo"], writes=["x1s"])
            fw.barrier()


NH = NT + 1


def build_program():
    nc = bass.Bass("TRN2", target_bir_lowering=False)
    A = []
    for l in range(DEPTH):
        a = {}

        def inp(name, shape, dt=F32, a=a, l=l):
            a[name] = nc.dram_tensor("%s_%d" % (name, l), list(shape), dt, kind="ExternalInput").ap()

        inp("wmod", [D, 6 * D])
        inp("bmod", [1, 6 * D])
        inp("lng", [2, D])
        inp("lnb", [2, D])
        inp("wq", [D, 2048])
        inp("kk", [16, 128, 128])
        inp("pu", [16384, D])
        inp("pv", [16384, D])
        if l % 2 == 0:
            nq = (SEQ + CTX) if l == 0 else NH * 128
            inp("w_in", [D, 704])
            inp("qng", [1, 384])
            inp("kvng", [1, 256])
            inp("w_uq", [384, 1536])
            inp("w_ukv", [256, 2048])
            inp("w_o", [D, D])
            inp("ropeq", [64, 2, nq])
        else:
            inp("pw_in", [D, D])
            inp("pw_grp", [4, 256, 256])
            inp("pscale", [128, 8])
            inp("pw_out", [D, D])
            inp("band", [128, 4, 9, 128])
        A.append(a)
    xfull = nc.dram_tensor("xfull", [SEQ, D], F32, kind="ExternalInput").ap()
    ctx0 = nc.dram_tensor("ctx", [CTX, D], F32, kind="ExternalInput").ap()
    ccol = nc.dram_tensor("ccol", [128, 16], F32, kind="ExternalInput").ap()
    ropek = nc.dram_tensor("ropek", [SEQ + CTX, 64], F32, kind="ExternalInput").ap()
    idx2_d = nc.dram_tensor("idx2", [128, NH], F32, kind="ExternalInput").ap()
    xout = nc.dram_tensor("xout", [TOK, D], F32, kind="ExternalOutput").ap()
    for a in A:
        a["ccol"] = ccol
        a["ropek"] = ropek

    def scratch(name, shape, dt=F32):
        return nc.dram_tensor(name, list(shape), dt, kind="Internal").ap()

    modrows = [scratch("modrows%d" % l, [2, 6 * D]) for l in range(DEPTH)]
    X1 = scratch("X1", [SEQ, D])
    C1 = scratch("C1", [CTX, D])
    X2 = scratch("X2", [SEQ, D])
    C2 = scratch("C2", [CTX, D])
    X3 = scratch("X3", [NH * 128, D])
    x1s = [scratch("x1s0", [SEQ + CTX, D]), scratch("x1s1", [SEQ + CTX, D]), scratch("x1s2", [NH * 128, D]), scratch("x1s3", [TOK, D])]
    attn0 = scratch("attn0", [8, 128, SEQ + CTX], BF16)
    cq0 = scratch("cq0", [128, 3, SEQ + CTX], BF16)
    attn2 = scratch("attn2", [8, 128, NH * 128], BF16)
    cq2 = scratch("cq2", [128, 3, NH * 128], BF16)
    NS = SEQ // 128
    uv_bf = scratch("uv_bf", [16384, 2 * D], BF16)
    with ExitStack() as es:
        p = Prog(nc, es)
        fw = p.fw
        idx2 = p.sb(es, "idx2", [128, NH], U32)
        idx2f = p.sb(es, "idx2f", [128, NH], F32)
        fw.dma("sp", lambda: nc.sync.dma_start(out=idx2f[:], in_=idx2_d[:, :]), p.dsem("misc"), sync=True, writes=["idx2f"])
        fw.op("dve", lambda: nc.vector.tensor_copy(idx2[:], idx2f[:]), reads=["idx2f"], writes=["idx2"])

        def plain_loader(src):
            def f(t, xt, xkey):
                fw.dma("sp", lambda: nc.sync.dma_start(out=xt[:], in_=src[t * 128:(t + 1) * 128, :]), p.dsem(xkey), reads=["xsrc"], writes=[xkey])
            return f

        def gather_loader(src):
            def f(t, xt, xkey):
                fw.dma("pool", lambda: nc.gpsimd.indirect_dma_start(out=xt[:], out_offset=None, in_=src[:, :],
                                                                    in_offset=bass.IndirectOffsetOnAxis(ap=idx2[:, t:t + 1], axis=0)),
                       p.dsem(xkey), reads=["xsrc", "idx2"], writes=[xkey])
            return f

        def dst_rows(xt_, ct_, n_x):
            def f(t):
                if t < n_x:
                    return xt_[t * 128:(t + 1) * 128, :], "xdst"
                return ct_[(t - n_x) * 128:(t - n_x + 1) * 128, :], "cdst"
            return f

        p.phase_mod(A[0], modrows[0])
        p.phase_mla(A[0], modrows[0], plain_loader(xfull), NS, xfull, ctx0, x1s[0], attn0, cq0, True)
        p.phase_cvt(A[0], uv_bf)
        p.phase_peer(A[0], modrows[0], x1s[0], NS, dst_rows(X1, C1, NS), True, uv_bf)
        p.phase_mod(A[1], modrows[1])
        p.phase_pool(A[1], modrows[1], plain_loader(X1), NS, None, C1, x1s[1], True)
        p.phase_cvt(A[1], uv_bf)
        p.phase_peer(A[1], modrows[1], x1s[1], NS, dst_rows(X2, C2, NS), True, uv_bf)
        p.phase_mod(A[2], modrows[2])
        p.phase_mla(A[2], modrows[2], gather_loader(X2), NH, X2, C2, x1s[2], attn2, cq2, False)
        p.phase_cvt(A[2], uv_bf)
        p.phase_peer(A[2], modrows[2], x1s[2], NH, dst_rows(X3, None, NH), False, uv_bf)
        p.phase_mod(A[3], modrows[3])
        p.phase_pool(A[3], modrows[3], plain_loader(X3), NT, X3[TOK:TOK + 16, :], None, x1s[3], False)
        p.phase_cvt(A[3], uv_bf)
        p.phase_peer(A[3], modrows[3], x1s[3], NT, dst_rows(xout, None, NT), False, uv_bf)
        fw.barrier(["sp"])
    return nc


def _rope_tables():
    rows = SEQ // 64
    row = np.repeat(np.arange(rows), 64).astype(np.float32)
    col = np.tile(np.arange(64), rows).astype(np.float32)
    freqs = (np.float32(10000.0) ** (-np.arange(16, dtype=np.float32) / np.float32(16))).astype(np.float32)
    ang = np.concatenate([row[:, None] * freqs, col[:, None] * freqs], axis=-1).astype(np.float32)
    return np.cos(ang).astype(np.float32), np.sin(ang).astype(np.float32)


def _band_mats(q):
    wins = (2, 4, 8, 16)
    band = np.zeros((128, 4, 9, 128), np.float32)

    def fill(g, L, t_glob0, kind):
        w = wins[g]
        m = np.zeros((128, 128), np.float32)
        for t in range(128):
            tg = t_glob0 + t
            lo = max(tg - w // 2, 0)
            hi = min(tg + w // 2, L)
            cnt = hi - lo
            for tp in range(lo, hi):
                r = tp - (t_glob0 + kind * 128)
                if 0 <= r < 128:
                    m[r, t] += 1.0 / cnt
            if kind == 0:
                m[t, t] -= 1.0
        return m

    for g in range(4):
        mid0 = 128 * 10
        band[:, g, 0, :] = fill(g, SEQ, mid0, 0)
        band[:, g, 1, :] = fill(g, SEQ, mid0, -1)
        band[:, g, 2, :] = fill(g, SEQ, mid0, +1)
        band[:, g, 3, :] = fill(g, SEQ, 0, 0) if q == 0 else band[:, g, 0, :]
        band[:, g, 4, :] = fill(g, SEQ, SEQ - 128, 0) if q == 3 else band[:, g, 0, :]
        band[:, g, 5, :] = fill(g, CTX, 0, 0)
        band[:, g, 6, :] = fill(g, CTX, CTX - 128, 0)
        if q != 0:
            band[0:8, g, 7, :] = band[120:128, g, 1, :]
        if q != 3:
            band[8:16, g, 8, :] = band[0:8, g, 2, :]
    return band


def _col_layout(v):
    return np.ascontiguousarray(v.reshape(8, 128).T)


def _band_full():
    b = _band_mats(0)
    b[:, :, 4] = _band_mats(3)[:, :, 4]
    b[:, :, 7:9] = 0.0
    return b


def kernel(x, c, ctx, c_ctx, w_mod, b_mod, ln_g, ln_b, mla_w_in, mla_q_norm, mla_kv_norm, mla_w_uq, mla_w_ukv, mla_w_o,
           pool_w_in, pool_w_grp, pool_scale, pool_w_out, peer_w_q, peer_k1, peer_k2, peer_u, peer_v):
    f = lambda t: np.ascontiguousarray(np.asarray(t, dtype=np.float32))
    x = f(x)
    ctx = f(ctx)
    cos, sin = _rope_tables()
    ropek = np.zeros((SEQ + CTX, 64), np.float32)
    ropek[:SEQ, :32] = cos
    ropek[:SEQ, 32:] = sin
    ropek[SEQ:, :32] = 1.0

    def rope_q(pos, n_extra):
        n = len(pos)
        rq = np.zeros((64, 2, n + n_extra), np.float32)
        rq[0:32, 0, :n] = cos[pos].T
        rq[32:64, 0, :n] = cos[pos].T
        rq[0:32, 1, :n] = sin[pos].T
        rq[32:64, 1, :n] = sin[pos].T
        rq[:, 0, n:] = 1.0
        return rq

    shared = {}
    for l in range(DEPTH):
        j = l // 2
        d = {
            "wmod": f(w_mod[l]), "bmod": f(b_mod[l])[None, :], "lng": f(ln_g[l]), "lnb": f(ln_b[l]), "wq": f(peer_w_q[l]),
            "kk": np.ascontiguousarray(np.stack([f(peer_k1[l]), f(peer_k2[l])], axis=1).reshape(16, 128, 128)),
            "pu": f(peer_u[l]), "pv": f(peer_v[l]),
        }
        if l % 2 == 0:
            d.update({"w_in": f(mla_w_in[j]), "qng": f(mla_q_norm[j])[None, :], "kvng": f(mla_kv_norm[j])[None, :],
                      "w_uq": f(mla_w_uq[j]), "w_ukv": f(mla_w_ukv[j]), "w_o": f(mla_w_o[j])})
        else:
            d.update({"pw_in": f(pool_w_in[j]), "pw_grp": f(pool_w_grp[j]), "pscale": _col_layout(f(pool_scale[j])), "pw_out": f(pool_w_out[j])})
        for k, v in d.items():
            shared["%s_%d" % (k, l)] = v
    shared["ropek"] = ropek
    shared["ropeq_0"] = rope_q(np.arange(SEQ), CTX)
    shared["band_1"] = _band_full()
    in_maps = []
    for core in range(NCORE):
        b, q = core // 4, core % 4
        idx = np.zeros((128, NH), np.int64)
        for t in range(NT):
            idx[:, t] = q * TOK + t * 128 + np.arange(128)
        idx[:, NT] = q * TOK
        idx[0:8, NT] = np.clip(q * TOK - 8 + np.arange(8), 0, SEQ - 1)
        idx[8:16, NT] = np.clip((q + 1) * TOK + np.arange(8), 0, SEQ - 1)
        m = dict(shared)
        m.update({
            "xfull": x[b], "ctx": ctx[b],
            "ccol": np.concatenate([_col_layout(f(c)[b]), _col_layout(f(c_ctx))], axis=1),
            "idx2": idx.astype(np.float32),
            "ropeq_2": rope_q(idx.T.reshape(-1), 0),
            "band_3": _band_mats(q),
        })
        in_maps.append(m)
    nc = build_program()
    res = run_bass_kernel_spmd(nc, in_maps, core_ids=list(range(NCORE)))
    out = np.empty_like(x)
    for core in range(NCORE):
        b, q = core // 4, core % 4
        out[b, q * TOK:(q + 1) * TOK] = res.results[core]["xout"]
    return out
```
